# Optimizing a Trainium2 kernel written in Bass

```python
import math
import jax, jax.numpy as jnp
from jax import lax
import numpy as np

D_MODEL = 1024
BATCH = 4
SEQ = 8192
DEPTH = 2

GRID_W = 64
CTX_LEN = 256
D_FF = 4 * D_MODEL
N_MOD = 6

MLA_HEADS = 8
QK_NOPE = 64
QK_ROPE = 32
QK_DIM = QK_NOPE + QK_ROPE
V_DIM = 64
MLA_WIDTH = MLA_HEADS * V_DIM
Q_LORA = 256
KV_LORA = 128
AXIS_DIM = QK_ROPE // 2
ROPE_THETA = 10000.0
Q_BLOCK = 128
SM_SCALE = 1.0 / math.sqrt(QK_DIM)

CONF_WIDTH = 256
CONF_K = 31
SC_WIDTH = 256
SC_K = 3

MIX_WIDTH = MLA_WIDTH + CONF_WIDTH + SC_WIDTH

Q_END = Q_LORA
KV_START = Q_END
KV_END = KV_START + KV_LORA
ROPE_END = KV_END + QK_ROPE
CONF_END = ROPE_END + 2 * CONF_WIDTH
IN_WIDTH = CONF_END + 3 * SC_WIDTH

EPS = 1e-6

kernel_name = "hybrid_mla_conformer_shortconv_dit"


def rmsnorm(u, g):
    uf = u.astype(jnp.float32)
    y = uf * lax.rsqrt(jnp.mean(uf * uf, axis=-1, keepdims=True) + EPS)
    return (y * g.astype(jnp.float32)).astype(u.dtype)


def layernorm(u, g, b):
    uf = u.astype(jnp.float32)
    mu = jnp.mean(uf, axis=-1, keepdims=True)
    d = uf - mu
    y = d * lax.rsqrt(jnp.mean(d * d, axis=-1, keepdims=True) + EPS)
    return (y * g.astype(jnp.float32) + b.astype(jnp.float32)).astype(u.dtype)


def modulation(cvec, w_mod, b_mod):
    m = jax.nn.silu(cvec) @ w_mod + b_mod
    return jnp.split(m[:, None, :], N_MOD, axis=-1)


def modulate(h, shift, scale):
    return h * (1.0 + scale) + shift


def axial_rope_tables(rows):
    row = jnp.broadcast_to(jnp.arange(rows)[:, None], (rows, GRID_W)).reshape(-1).astype(jnp.float32)
    col = jnp.broadcast_to(jnp.arange(GRID_W)[None, :], (rows, GRID_W)).reshape(-1).astype(jnp.float32)
    inv = 1.0 / (ROPE_THETA ** (jnp.arange(0, AXIS_DIM, 2, dtype=jnp.float32) / AXIS_DIM))
    ar = row[:, None] * inv
    ac = col[:, None] * inv
    ang = jnp.concatenate([ar, ar, ac, ac], axis=-1)
    return jnp.cos(ang), jnp.sin(ang)


def apply_axial_rope(u, cos, sin):
    def rot(p):
        h = p.shape[-1] // 2
        return jnp.concatenate([-p[..., h:], p[..., :h]], axis=-1)
    rotated = jnp.concatenate([rot(u[..., :AXIS_DIM]), rot(u[..., AXIS_DIM:])], axis=-1)
    return (u.astype(jnp.float32) * cos + rotated.astype(jnp.float32) * sin).astype(u.dtype)


def mla_queries(zq, g_q, w_q_b, rope):
    b, s, _ = zq.shape
    q = (rmsnorm(zq, g_q) @ w_q_b).reshape(b, s, MLA_HEADS, QK_DIM)
    if rope is None:
        return q
    cos, sin = rope
    q_rope = apply_axial_rope(q[..., QK_NOPE:], cos[:, None, :], sin[:, None, :])
    return jnp.concatenate([q[..., :QK_NOPE], q_rope], axis=-1)


def mla_keys_values(zkv, g_kv, w_kv_b, rope):
    b, s, _ = zkv.shape
    ckv, k_rope = zkv[..., :KV_LORA], zkv[..., KV_LORA:]
    kv = (rmsnorm(ckv, g_kv) @ w_kv_b).reshape(b, s, MLA_HEADS, QK_NOPE + V_DIM)
    k_nope, v = kv[..., :QK_NOPE], kv[..., QK_NOPE:]
    if rope is not None:
        cos, sin = rope
        k_rope = apply_axial_rope(k_rope, cos, sin)
    k_rope = jnp.broadcast_to(k_rope[:, :, None, :], (b, s, MLA_HEADS, QK_ROPE))
    return jnp.concatenate([k_nope, k_rope], axis=-1), v


def block_attention(q, k, v):
    b, s, h, dq = q.shape
    dv = v.shape[-1]
    nb = s // Q_BLOCK
    qb = jnp.moveaxis(q.reshape(b, nb, Q_BLOCK, h, dq), 1, 0)

    def attend(qblk):
        logits = jnp.einsum('bqhd,bkhd->bhqk', qblk, k).astype(jnp.float32) * SM_SCALE
        p = jax.nn.softmax(logits, axis=-1).astype(v.dtype)
        return jnp.einsum('bhqk,bkhd->bqhd', p, v)

    o = lax.map(attend, qb)
    return jnp.moveaxis(o, 0, 1).reshape(b, s, h * dv)


def depthwise_conv(u, w):
    k, ch = w.shape
    return lax.conv_general_dilated(
        u, w[:, None, :].astype(u.dtype), window_strides=(1,),
        padding=[(k // 2, k // 2)], dimension_numbers=('NWC', 'WIO', 'NWC'),
        feature_group_count=ch)


def conformer_branch(z, dw_w, dw_b, ln_g, ln_b):
    a, g = jnp.split(z, 2, axis=-1)
    u = a * jax.nn.sigmoid(g)
    u = depthwise_conv(u, dw_w) + dw_b
    u = layernorm(u, ln_g, ln_b)
    return jax.nn.silu(u)


def shortconv_branch(z, w):
    b_gate, c_gate, h = jnp.split(z, 3, axis=-1)
    return b_gate * depthwise_conv(c_gate * h, w)


def mix_stream(z, k, v, rope, g_q, w_q_b, conf_dw_w, conf_dw_b, conf_ln_g, conf_ln_b,
               sc_dw_w, g_branch, w_o):
    q = mla_queries(z[..., :Q_END], g_q, w_q_b, rope)
    attn = block_attention(q, k, v)
    conf = conformer_branch(z[..., ROPE_END:CONF_END], conf_dw_w, conf_dw_b, conf_ln_g, conf_ln_b)
    sc = shortconv_branch(z[..., CONF_END:], sc_dw_w)
    merged = jnp.concatenate([
        rmsnorm(attn, g_branch[:MLA_WIDTH]),
        rmsnorm(conf, g_branch[MLA_WIDTH:MLA_WIDTH + CONF_WIDTH]),
        rmsnorm(sc, g_branch[MLA_WIDTH + CONF_WIDTH:])], axis=-1)
    return merged @ w_o


def sqrelu_mlp(h, w1, w2):
    a = jax.nn.relu(h @ w1)
    return (a * a) @ w2


def setup_inputs(seed: int = 0) -> dict:
    key = jax.random.key(seed)
    ks = jax.random.split(key, 32)
    f32 = jnp.float32

    def nrm(k, shape, scale):
        return jax.random.normal(k, shape, f32) * scale

    def gain(k, shape):
        return 1.0 + 0.05 * jax.random.normal(k, shape, f32)

    L, D = DEPTH, D_MODEL
    return {
        "x": nrm(ks[0], (BATCH, SEQ, D), 1.0),
        "c": nrm(ks[1], (BATCH, D), 1.0),
        "ctx": nrm(ks[2], (BATCH, CTX_LEN, D), 1.0),
        "c_ctx": nrm(ks[3], (D,), 1.0),
        "w_mod": nrm(ks[4], (L, D, N_MOD * D), 0.5 * D ** -0.5),
        "b_mod": nrm(ks[5], (L, N_MOD * D), 0.02),
        "g_pre_mix": gain(ks[6], (L, D)),
        "g_post_mix": gain(ks[7], (L, D)),
        "g_pre_mlp": gain(ks[8], (L, D)),
        "g_post_mlp": gain(ks[9], (L, D)),
        "w_in": nrm(ks[10], (L, D, IN_WIDTH), D ** -0.5),
        "g_q": gain(ks[11], (L, Q_LORA)),
        "w_q_b": nrm(ks[12], (L, Q_LORA, MLA_HEADS * QK_DIM), Q_LORA ** -0.5),
        "g_kv": gain(ks[13], (L, KV_LORA)),
        "w_kv_b": nrm(ks[14], (L, KV_LORA, MLA_HEADS * (QK_NOPE + V_DIM)), KV_LORA ** -0.5),
        "conf_dw_w": nrm(ks[15], (L, CONF_K, CONF_WIDTH), CONF_K ** -0.5),
        "conf_dw_b": nrm(ks[16], (L, CONF_WIDTH), 0.02),
        "conf_ln_g": gain(ks[17], (L, CONF_WIDTH)),
        "conf_ln_b": nrm(ks[18], (L, CONF_WIDTH), 0.02),
        "sc_dw_w": nrm(ks[19], (L, SC_K, SC_WIDTH), SC_K ** -0.5),
        "g_branch": gain(ks[20], (L, MIX_WIDTH)),
        "w_o": nrm(ks[21], (L, MIX_WIDTH, D), MIX_WIDTH ** -0.5),
        "w_mlp_in": nrm(ks[22], (L, D, D_FF), D ** -0.5),
        "w_mlp_out": nrm(ks[23], (L, D_FF, D), D_FF ** -0.5),
    }


def reference(x, c, ctx, c_ctx, w_mod, b_mod, g_pre_mix, g_post_mix, g_pre_mlp, g_post_mlp,
              w_in, g_q, w_q_b, g_kv, w_kv_b, conf_dw_w, conf_dw_b, conf_ln_g, conf_ln_b,
              sc_dw_w, g_branch, w_o, w_mlp_in, w_mlp_out):
    n_lat = x.shape[1]
    rows = n_lat // GRID_W
    rope = axial_rope_tables(rows)
    xl, xc = x, ctx
    for i in range(DEPTH):
        last = i == DEPTH - 1
        ml = modulation(c, w_mod[i], b_mod[i])
        mc = modulation(c_ctx[None, :], w_mod[i], b_mod[i])

        hl = modulate(rmsnorm(xl, g_pre_mix[i]), ml[0], ml[1])
        hc = modulate(rmsnorm(xc, g_pre_mix[i]), mc[0], mc[1])
        zl = hl @ w_in[i]
        if last:
            zc_kv = hc @ w_in[i][:, KV_START:ROPE_END]
        else:
            zc = hc @ w_in[i]
            zc_kv = zc[..., KV_START:ROPE_END]

        k_c, v_c = mla_keys_values(zc_kv, g_kv[i], w_kv_b[i], None)
        k_l, v_l = mla_keys_values(zl[..., KV_START:ROPE_END], g_kv[i], w_kv_b[i], rope)
        y_l = mix_stream(zl, jnp.concatenate([k_c, k_l], axis=1), jnp.concatenate([v_c, v_l], axis=1),
                         rope, g_q[i], w_q_b[i], conf_dw_w[i], conf_dw_b[i], conf_ln_g[i],
                         conf_ln_b[i], sc_dw_w[i], g_branch[i], w_o[i])
        xl_new = xl + ml[2] * rmsnorm(y_l, g_post_mix[i])

        h2 = modulate(rmsnorm(xl_new, g_pre_mlp[i]), ml[3], ml[4])
        xl_new = xl_new + ml[5] * rmsnorm(sqrelu_mlp(h2, w_mlp_in[i], w_mlp_out[i]), g_post_mlp[i])

        if not last:
            y_c = mix_stream(zc, k_c, v_c, None, g_q[i], w_q_b[i], conf_dw_w[i], conf_dw_b[i],
                             conf_ln_g[i], conf_ln_b[i], sc_dw_w[i], g_branch[i], w_o[i])
            xc_new = xc + mc[2] * rmsnorm(y_c, g_post_mix[i])
            h2c = modulate(rmsnorm(xc_new, g_pre_mlp[i]), mc[3], mc[4])
            xc = xc_new + mc[5] * rmsnorm(sqrelu_mlp(h2c, w_mlp_in[i], w_mlp_out[i]), g_post_mlp[i])
        xl = xl_new
    return xl
```

```python
import math
from contextlib import ExitStack

import numpy as np
import concourse.bass as bass
import concourse.mybir as mybir
from concourse.bass_utils import run_bass_kernel_spmd

F32 = mybir.dt.float32
F32R = mybir.dt.float32r
BF16 = mybir.dt.bfloat16
AF = mybir.ActivationFunctionType
ALU = mybir.AluOpType

D = 1024
SEQ = 8192
NB = 4
DEPTH = 2
CTX = 256
OWN = 4096
NTOK = OWN + CTX
NKEY = CTX + SEQ
NKT = NKEY // 128
HEADS = 8
IN_W = 1696
DFF = 4096
EPS = 1e-6
SM_SCALE = 1.0 / math.sqrt(96.0)
HALO = 15
LAT0 = HALO
CTX0 = HALO + OWN + HALO + HALO
TP = CTX0 + CTX + HALO

V_GPRE1, V_GPOST1, V_GPRE2, V_GPOST2 = 0, 8, 16, 24
V_BMOD = 32
V_GQ = 80
V_GKV = 82
V_DWB = 83
V_LNG = 85
V_LNB = 87
V_GBC = 89
V_GBS = 91
V_GBA = 93
V_CW = 101
V_SW = 163
NV = 169

SAME_ENG_SYNC = True


class Res:
    __slots__ = ("w", "rs")

    def __init__(self):
        self.w = None
        self.rs = {}


class Op:
    __slots__ = ("eng", "fn", "deps", "sig", "cnt", "chan", "ndma")


class Rec:
    __slots__ = ("name", "args", "kw")

    def __init__(self, name, args, kw):
        self.name, self.args, self.kw = name, args, kw


class _Proxy:
    def __getattr__(self, name):
        return lambda *a, **kw: Rec(name, a, kw)


PROXY = _Proxy()


class Tile:
    def __init__(self, t, nsub=1):
        self.t = t
        self.res = [Res() for _ in range(nsub)]

    def R(self, *idx):
        if not idx:
            return list(self.res)
        return [self.res[i] for i in idx]


class Sched:
    ENGS = ("sp", "pe", "act", "dve", "pool")

    def __init__(self):
        self.ops = {e: [] for e in self.ENGS}
        self.last = {}
        self.pending = {e: [] for e in self.ENGS}
        self.nchan = 0

    def new_chan(self):
        self.nchan += 1
        return "c%d" % self.nchan

    def _key(self, o):
        return o.chan if o.chan is not None else o.eng

    def op(self, eng, fn, reads=(), writes=(), chan=None, ndma=0):
        o = Op()
        o.eng, o.fn, o.chan, o.ndma, o.sig, o.cnt = eng, fn(PROXY), chan, ndma, chan is not None, 0
        deps = {}

        def add(d):
            if d is None:
                return
            if d.chan is None and d.eng == eng and (eng == "pe" or not SAME_ENG_SYNC):
                return
            deps[id(d)] = d

        for r in reads:
            add(r.w)
        for w in writes:
            add(w.w)
            for x in w.rs.values():
                add(x)
        for d in self.pending[eng]:
            add(d)
        self.pending[eng] = []
        o.deps = list(deps.values())
        for d in o.deps:
            d.sig = True
        k = self._key(o)
        for r in reads:
            r.rs[k] = o
        for w in writes:
            w.w = o
            w.rs = {}
        self.ops[eng].append(o)
        self.last[k] = o
        return o

    def barrier(self):
        lasts = list(self.last.values())
        for e in self.ENGS:
            self.pending[e] = list(lasts)

    def emit(self, nc, stack, final_ops):
        for o in final_ops:
            o.sig = True
        cnts = {}
        for e in self.ENGS:
            for o in self.ops[e]:
                k = self._key(o)
                if o.chan is not None:
                    cnts[k] = cnts.get(k, 0) + 16 * o.ndma
                    o.cnt = cnts[k]
                elif o.sig:
                    cnts[k] = cnts.get(k, 0) + 1
                    o.cnt = cnts[k]
        print("n_sems", len(cnts), {e: len(self.ops[e]) for e in self.ENGS})
        sems = {k: stack.enter_context(nc.semaphore("s_" + k)) for k in cnts}
        block = stack.enter_context(nc.Block())

        def run(ename, eng):
            known = {}

            def wait_for(dlist):
                need = {}
                for d in dlist:
                    k = self._key(d)
                    if d.cnt > need.get(k, 0):
                        need[k] = d.cnt
                for k, v in need.items():
                    if known.get(k, 0) < v:
                        eng.wait_ge(sems[k], v)
                        known[k] = v

            for o in self.ops[ename]:
                wait_for(o.deps)
                k = self._key(o)
                if o.chan is not None:
                    assert len(o.fn) == o.ndma, (len(o.fn), o.ndma)
                    for r in o.fn:
                        getattr(eng, r.name)(*r.args, **r.kw).then_inc(sems[k], 16)
                else:
                    r = o.fn
                    ins = getattr(eng, r.name)(*r.args, **r.kw)
                    if o.sig:
                        ins.then_inc(sems[k], 1)
            if ename == "sp":
                wait_for(final_ops)

        block.sync(lambda e: run("sp", e))
        block.tensor(lambda e: run("pe", e))
        block.scalar(lambda e: run("act", e))
        block.vector(lambda e: run("dve", e))
        block.gpsimd(lambda e: run("pool", e))


def build_program(layers=(0, 1), phases=("mod", "A", "X", "ATT", "C1", "C2"), debug_ext=False,
                  stop_after=None):
    nc = bass.Bass("TRN2", target_bir_lowering=False)
    S = Sched()
    top = ExitStack()

    def dram(name, shape, dt, kind=None):
        if kind is None:
            return nc.dram_tensor(name, shape, dt).ap()
        return nc.dram_tensor(name, shape, dt, kind=kind).ap()

    xT = dram("xT", [D, NTOK], F32, "ExternalInput")
    cvecT = dram("cvecT", [128, 16], F32, "ExternalInput")
    vecs = dram("vecs", [DEPTH, 128, NV], F32, "ExternalInput")
    rope = dram("rope", [2, 128, NTOK], F32, "ExternalInput")
    consts = dram("consts", [128, 130], F32, "ExternalInput")
    w_mod = dram("w_mod", [DEPTH, D, 6 * D], F32, "ExternalInput")
    w_in = dram("w_in", [DEPTH, D, IN_W], F32, "ExternalInput")
    w_q_b = dram("w_q_b", [DEPTH, 256, 768], F32, "ExternalInput")
    w_kv_b = dram("w_kv_b", [DEPTH, 128, 1024], F32, "ExternalInput")
    w_o = dram("w_o", [DEPTH, D, D], F32, "ExternalInput")
    w1 = dram("w_mlp_in", [DEPTH, D, DFF], F32, "ExternalInput")
    w2 = dram("w_mlp_out", [DEPTH, DFF, D], F32, "ExternalInput")
    outT = dram("outT", [D, OWN], F32, "ExternalOutput")

    qT = dram("qT", [HEADS, 96, NTOK], BF16)
    lat_ctx = dram("lat_ctx", [160, CTX], BF16)
    convT = dram("convT", [768, TP], BF16)
    attnT = dram("attnT", [HEADS, 64, NTOK], F32)
    xmid = dram("xmid", [D, NTOK], F32)
    xl1 = dram("xl1", [D, NTOK], F32)
    send_lat = nc.dram_tensor("send_lat", [160, OWN], BF16).ap()
    recv_lat = nc.dram_tensor("recv_lat", [320, OWN], BF16).ap()
    send_edge = nc.dram_tensor("send_edge", [512, 2 * HALO], BF16).ap()
    recv_edge = nc.dram_tensor("recv_edge", [1024, 2 * HALO], BF16).ap()
    dbg_lat = None
    dbg_mod = dram("dbg_mod", [128, DEPTH * 6 * 16], F32, "ExternalOutput") if debug_ext else None
    dbgb = dram("dbgb", [DEPTH * 8, 128, 256], BF16, "ExternalOutput") if debug_ext else None
    dbgf = dram("dbgf", [DEPTH * 6, 128, 256], F32, "ExternalOutput") if debug_ext else None

    dbg_chan = S.new_chan()

    def dbg_copy(dst, src, reads):
        o = S.op("sp", lambda e: [e.dma_start(out=dst, in_=src)], reads=reads, chan=dbg_chan, ndma=1)
        final_ops.append(o)

    R_q, R_latctx, R_conv, R_attn, R_xmid, R_xl1, R_out = (Res() for _ in range(7))
    R_send, R_recv, R_sedge, R_redge = Res(), Res(), Res(), Res()

    name_i = [0]

    def sb(stack, name, shape, dt, nsub=1):
        name_i[0] += 1
        return Tile(stack.enter_context(nc.sbuf_tensor("%s_%d" % (name, name_i[0]), shape, dt)), nsub)

    banks = [Tile(top.enter_context(nc.psum_tensor("bank%d" % i, [128, 512], F32)), 2) for i in range(8)]
    bank_i = [0, 0]

    def next_bank():
        b = banks[bank_i[0] % 5]
        bank_i[0] += 1
        return b

    def acc_bank():
        b = banks[5 + bank_i[1] % 3]
        bank_i[1] += 1
        return b

    ones = sb(top, "ones", [128, 128], F32)
    sel = sb(top, "sel", [65, 64], F32)
    eye = sb(top, "eye", [128, 130], F32)
    eyeb = sb(top, "eyeb", [128, 128], BF16)
    vec = [sb(top, "vec%d" % l, [128, NV], F32) for l in range(DEPTH)]
    MOD = [sb(top, "mod%d" % l, [128, 6 * 16], F32) for l in range(DEPTH)]
    zero15 = sb(top, "zero15", [128, HALO], BF16)
    final_ops = []

    S.op("dve", lambda e: e.memset(ones.t[:], 1.0), writes=ones.R())
    onesr = sb(top, "onesr", [128, 128], F32R)
    S.op("dve", lambda e: e.tensor_copy(out=onesr.t[:], in_=ones.t[:]), reads=ones.R(), writes=onesr.R())
    S.op("dve", lambda e: e.memset(sel.t[:], 0.0), writes=sel.R())
    S.op("dve", lambda e: e.memset(sel.t[64:65, :], 1.0), writes=sel.R())
    S.op("dve", lambda e: e.memset(zero15.t[:], 0.0), writes=zero15.R())
    S.op("sp", lambda e: [e.dma_start(out=eye.t[:], in_=consts[:, :])], writes=eye.R(),
         chan=S.new_chan(), ndma=1)
    S.op("dve", lambda e: e.tensor_copy(out=eyeb.t[:], in_=eye.t[:, 0:128]), reads=eye.R(), writes=eyeb.R())
    for l in range(DEPTH):
        S.op("sp", lambda e, l=l: [e.dma_start(out=vec[l].t[:], in_=vecs[l])], writes=vec[l].R(),
             chan=S.new_chan(), ndma=1)

    def mod_ap(l, j, c, s):
        i = j * 16 + c * 2 + s
        return MOD[l].t[:, i:i + 1]

    def phase_mod():
        with ExitStack() as st:
            cv = sb(st, "cv", [128, 16], F32)
            scb = sb(st, "scb", [128, 8, 2], BF16)
            wmf = [sb(st, "wmf%d" % i, [128, 8, 1024], F32, 4) for i in range(2)]
            wm = [sb(st, "wm%d" % i, [128, 8, 1024], BF16, 4) for i in range(2)]
            mraw = sb(st, "mraw", [128, 6 * 16], F32)
            S.op("sp", lambda e: [e.dma_start(out=cv.t[:], in_=cvecT[:, :])], writes=cv.R(),
                 chan=S.new_chan(), ndma=1)
            S.op("act", lambda e: e.activation(out=scb.t[:].rearrange("p c s -> p (c s)"), in_=cv.t[:], func=AF.Silu),
                 reads=cv.R(), writes=scb.R())
            chans = [S.new_chan(), S.new_chan()]
            it = 0
            for l in layers:
                wv = w_mod[l].rearrange("(c p) n -> p c n", p=128)
                bank = acc_bank()
                for j in range(6):
                    w = wm[it % 2]
                    wf = wmf[it % 2]
                    S.op("sp", lambda e, wf=wf, j=j, wv=wv: [
                        e.dma_start(out=wf.t[:, 2 * q:2 * q + 2, :], in_=wv[:, 2 * q:2 * q + 2, j * 1024:(j + 1) * 1024])
                        for q in range(4)], writes=wf.R(), chan=chans[it % 2], ndma=4)
                    for q in range(4):
                        eng = ("dve", "pool", "act", "dve")[q]
                        if eng == "act":
                            S.op("act", lambda e: e.activation(out=w.t[:, 2 * q:2 * q + 2, :], in_=wf.t[:, 2 * q:2 * q + 2, :],
                                                               func=AF.Copy), reads=wf.R(q), writes=w.R(q))
                        else:
                            S.op(eng, lambda e: e.tensor_copy(out=w.t[:, 2 * q:2 * q + 2, :], in_=wf.t[:, 2 * q:2 * q + 2, :]),
                                 reads=wf.R(q), writes=w.R(q))
                    it += 1
                    for oc in range(8):
                        for k in range(8):
                            col = (j * 8 + oc) * 2
                            S.op("pe", lambda e, w=w, oc=oc, k=k, col=col, bank=bank: e.matmul(
                                bank.t[:, col:col + 2], lhsT=w.t[:, k, oc * 128:(oc + 1) * 128],
                                rhs=scb.t[:, k, :], start=(k == 0), stop=(k == 7)),
                                reads=w.R(k // 2) + scb.R(), writes=bank.R())
                for s in range(2):
                    S.op("dve", lambda e, l=l, s=s, bank=bank: e.tensor_tensor(
                        out=mraw.t[:].rearrange("p (q s) -> p q s", s=2)[:, :, s],
                        in0=bank.t[:, 0:96].rearrange("p (q s) -> p q s", s=2)[:, :, s],
                        in1=vec[l].t[:, V_BMOD:V_BMOD + 48], op=ALU.add),
                        reads=bank.R() + vec[l].R(), writes=mraw.R())
                M3 = MOD[l].t[:].rearrange("p (j c s) -> p j c s", j=6, s=2)
                R3 = mraw.t[:].rearrange("p (j c s) -> p j c s", j=6, s=2)
                for s in range(2):
                    for (dst, src_scale, gcol, kind) in ((0, 1, V_GPRE1, "g1"), (2, 2, V_GPOST1, "gp"),
                                                         (3, 4, V_GPRE2, "g1"), (5, 5, V_GPOST2, "gp")):
                        if kind == "g1":
                            S.op("dve", lambda e, l=l, s=s, dst=dst, src=src_scale, gcol=gcol, M3=M3, R3=R3:
                                 e.scalar_tensor_tensor(out=M3[:, dst, :, s], in0=R3[:, src, :, s], scalar=1.0,
                                                        in1=vec[l].t[:, gcol:gcol + 8], op0=ALU.add, op1=ALU.mult),
                                 reads=mraw.R() + vec[l].R(), writes=MOD[l].R())
                        else:
                            S.op("dve", lambda e, l=l, s=s, dst=dst, src=src_scale, gcol=gcol, M3=M3, R3=R3:
                                 e.tensor_tensor(out=M3[:, dst, :, s], in0=R3[:, src, :, s],
                                                 in1=vec[l].t[:, gcol:gcol + 8], op=ALU.mult),
                                 reads=mraw.R() + vec[l].R(), writes=MOD[l].R())
                    for (dst, src) in ((1, 0), (4, 3)):
                        S.op("dve", lambda e, s=s, dst=dst, src=src, M3=M3, R3=R3:
                             e.tensor_copy(out=M3[:, dst, :, s], in_=R3[:, src, :, s]),
                             reads=mraw.R(), writes=MOD[l].R())
                if debug_ext:
                    o = S.op("sp", lambda e, l=l: [e.dma_start(out=dbg_mod[:, l * 96:(l + 1) * 96], in_=MOD[l].t[:])],
                             reads=MOD[l].R(), chan=S.new_chan(), ndma=1)
                    final_ops.append(o)
        S.barrier()

    def emit_rstd(bank, tmp, out, np_, n, dim):
        S.op("act", lambda e: e.activation(out=tmp.t[0:np_, 0:n], in_=bank.t[0:np_, 0:n], func=AF.Sqrt,
                                           bias=epsb.t[0:np_, 0:1], scale=1.0 / dim),
             reads=bank.R() + epsb.R(), writes=tmp.R())
        S.op("dve", lambda e: e.reciprocal(out=out.t[0:np_, 0:n], in_=tmp.t[0:np_, 0:n]),
             reads=tmp.R(), writes=out.R())

    rscr = sb(top, "rscr", [128, 512], F32)
    epsb = sb(top, "epsb", [128, 1], F32)
    S.op("dve", lambda e: e.memset(epsb.t[:], EPS), writes=epsb.R())

    def chunks(n, with_ctx):
        cs = [(t0, n, 0) for t0 in range(0, OWN, n)]
        if with_ctx:
            cs += [(t0, min(n, CTX), 1) for t0 in range(OWN, NTOK, min(n, CTX))]
        return cs

    def conv_col(t0, s):
        return LAT0 + t0 if s == 0 else CTX0 + (t0 - OWN)

    def alloc_aw(stack):
        return (sb(stack, "win", [128, 8, IN_W], BF16), sb(stack, "wkp", [128, 8, 32], BF16),
                sb(stack, "wqn", [128, 2, 512], BF16), sb(stack, "wqr", [128, 2, 256], BF16),
                sb(stack, "wqp", [128, 2, 256], BF16))

    def load_aw(l, aw):
        win, wkp, wqn, wqr, wqp = aw
        wv = w_in[l].rearrange("(c p) n -> p c n", p=128)
        ch = S.new_chan()
        S.op("pool", lambda e: [e.dma_start(out=win.t[:, 0:4, :], in_=wv[:, 0:4, :]),
                                e.dma_start(out=win.t[:, 4:8, :], in_=wv[:, 4:8, :])],
             writes=win.R(), chan=ch, ndma=2)
        ch = S.new_chan()
        S.op("pool", lambda e: [e.dma_start(out=wkp.t[:, :, d0:d0 + 8], in_=wv[:, :, 384 + s0:384 + s0 + 8])
                                for (d0, s0) in ((0, 8), (8, 0), (16, 24), (24, 16))],
             writes=wkp.R(), chan=ch, ndma=4)
        qv = w_q_b[l].rearrange("(c p) (h d) -> p c h d", p=128, d=96)
        ch = S.new_chan()

        def load_wq(e):
            ins = []
            for c in range(2):
                ins.append(e.dma_start(out=wqn.t[:, c, :].rearrange("p (h d) -> p h d", d=64),
                                       in_=qv[:, c, :, 0:64]))
                ins.append(e.dma_start(out=wqr.t[:, c, :].rearrange("p (h d) -> p h d", d=32),
                                       in_=qv[:, c, :, 64:96]))
                for (d0, s0) in ((0, 8), (8, 0), (16, 24), (24, 16)):
                    ins.append(e.dma_start(
                        out=wqp.t[:, c, :].rearrange("p (h d) -> p h d", d=32)[:, :, d0:d0 + 8],
                        in_=qv[:, c, :, 64 + s0:64 + s0 + 8]))
            return ins
        S.op("pool", load_wq, writes=wqn.R() + wqr.R() + wqp.R(), chan=ch, ndma=12)

    def phase_A(l, xsrc, R_x, aw=None):
        with ExitStack() as st:
            if aw is None:
                aw = alloc_aw(st)
                load_aw(l, aw)
            win, wkp, wqn, wqr, wqp = aw
            xc = [sb(st, "xc%d" % i, [128, 8, 512], F32, 8) for i in range(2)]
            sq = sb(st, "sqA", [128, 8, 512], F32R, 8)
            tmpn = [sb(st, "tmpn%d" % i, [128, 512], F32) for i in range(2)]
            hT = [sb(st, "hT%d" % i, [128, 8, 512], BF16, 8) for i in range(2)]
            rs_t = sb(st, "rsA_t", [128, 512], F32)
            rs_t2 = sb(st, "rsA_t2", [128, 512], F32)
            rs_n = sb(st, "rsA_n", [128, 512], F32)
            rstd = sb(st, "rstdA", [128, 512], F32)
            rsq = sb(st, "rstdq", [128, 512], F32)
            rskv = sb(st, "rstdkv", [128, 512], F32)
            zq = sb(st, "zq", [128, 2, 512], F32, 2)
            zqs = sb(st, "zqs", [128, 2, 512], F32R, 2)
            zqn = sb(st, "zqn", [128, 2, 512], BF16, 2)
            ckv = sb(st, "ckv", [128, 512], F32)
            ckvs = sb(st, "ckvs", [128, 512], F32R)
            lat = [sb(st, "lat%d" % i, [128, 512], BF16) for i in range(2)]
            kro = [sb(st, "kro%d" % i, [32, 512], BF16) for i in range(2)]
            t1 = sb(st, "t1A", [128, 512], F32)
            t2 = sb(st, "t2A", [128, 512], F32)
            qn = [sb(st, "qn%d" % i, [128, 4, 512], BF16, 4) for i in range(2)]
            qr = [sb(st, "qr%d" % i, [128, 2, 512], BF16, 2) for i in range(2)]
            cs_t = [sb(st, "cos%d" % i, [128, 2, 512], F32) for i in range(2)]
            sig = sb(st, "sig", [128, 512], F32)
            csb = sb(st, "csb", [128, 512], F32)
            upb = [sb(st, "upb%d" % i, [128, 6, 512], BF16, 6) for i in range(2)]


            xv = xsrc.rearrange("(c p) t -> p c t", p=128)
            cl = chunks(512, True)
            chx = [S.new_chan(), S.new_chan()]
            chc = [S.new_chan(), S.new_chan()]
            ch_lat = [S.new_chan() for _ in range(2)]
            ch_q = [S.new_chan() for _ in range(2)]
            ch_up = [S.new_chan() for _ in range(2)]

            def load_x(i):
                t0, n, s = cl[i]
                S.op("sp", lambda e: [e.dma_start(out=xc[i % 2].t[:, 0:4, 0:n], in_=xv[:, 0:4, t0:t0 + n]),
                                      e.dma_start(out=xc[i % 2].t[:, 4:8, 0:n], in_=xv[:, 4:8, t0:t0 + n])],
                     reads=[R_x], writes=xc[i % 2].R(), chan=chx[i % 2], ndma=2)
                S.op("sp", lambda e: [e.dma_start(out=cs_t[i % 2].t[:, :, 0:n],
                                                  in_=rope[:, :, t0:t0 + n].rearrange("a p t -> p a t"))],
                     writes=cs_t[i % 2].R(), chan=chc[i % 2], ndma=1)

            def norm(i):
                t0, n, s = cl[i]
                X, H = xc[i % 2], hT[i % 2]
                bss = acc_bank()

                def ssx(c):
                    S.op("pe", lambda e: e.matmul(bss.t[:, 0:n], lhsT=onesr.t[:], rhs=sq.t[:, c, 0:n],
                                                  start=(c == 0), stop=(c == 7)), reads=onesr.R() + sq.R(c), writes=bss.R())
                for c in range(8):
                    if c >= 2:
                        ssx(c - 2)
                    S.op("pool", lambda e: e.tensor_tensor(out=sq.t[:, c, 0:n], in0=X.t[:, c, 0:n], in1=X.t[:, c, 0:n],
                                                           op=ALU.mult), reads=X.R(c), writes=sq.R(c))
                ssx(6)
                ssx(7)
                emit_rstd(bss, rs_n, rstd, 128, n, float(D))
                for c in range(8):
                    tm = tmpn[c % 2]
                    S.op("dve", lambda e: e.tensor_tensor(out=tm.t[:, 0:n], in0=X.t[:, c, 0:n], in1=rstd.t[:, 0:n], op=ALU.mult),
                         reads=X.R(c) + rstd.R(), writes=tm.R())
                    S.op("dve", lambda e: e.tensor_scalar(out=H.t[:, c, 0:n], in0=tm.t[:, 0:n], scalar1=mod_ap(l, 0, c, s),
                                                          scalar2=mod_ap(l, 1, c, s), op0=ALU.mult, op1=ALU.add),
                         reads=tm.R() + MOD[l].R(), writes=H.R(c))

            def body(i):
                t0, n, s = cl[i]
                X, H, CS = xc[i % 2], hT[i % 2], cs_t[i % 2]
                need_rest = not (s == 1 and l == DEPTH - 1)

                def proj(cols, wt=None, m=None):
                    b = next_bank()
                    for k in range(8):
                        if wt is None:
                            lhs, rd, mm = win.t[:, k, cols[0]:cols[1]], win.R(), cols[1] - cols[0]
                        else:
                            lhs, rd, mm = wt.t[:, k, :], wt.R(), m
                        S.op("pe", lambda e: e.matmul(b.t[0:mm, 0:n], lhsT=lhs, rhs=H.t[:, k, 0:n], start=(k == 0), stop=(k == 7)),
                             reads=rd + H.R(k), writes=b.R())
                    return b
                for c in range(2):
                    b = proj((c * 128, (c + 1) * 128))
                    S.op("act", lambda e: e.activation(out=zq.t[:, c, 0:n], in_=b.t[:, 0:n], func=AF.Copy),
                         reads=b.R(), writes=zq.R(c))
                    S.op("pool", lambda e: e.tensor_tensor(out=zqs.t[:, c, 0:n], in0=zq.t[:, c, 0:n], in1=zq.t[:, c, 0:n],
                                                           op=ALU.mult), reads=zq.R(c), writes=zqs.R(c))
                b = proj((256, 384))
                S.op("act", lambda e: e.activation(out=ckv.t[:, 0:n], in_=b.t[:, 0:n], func=AF.Copy), reads=b.R(), writes=ckv.R())
                S.op("act", lambda e: e.activation(out=ckvs.t[:, 0:n], in_=ckv.t[:, 0:n], func=AF.Square),
                     reads=ckv.R(), writes=ckvs.R())
                b1 = proj((384, 416))
                b2 = proj(None, wt=wkp, m=32)
                KR = kro[i % 2]
                S.op("dve", lambda e: e.tensor_tensor(out=t1.t[0:32, 0:n], in0=b1.t[0:32, 0:n], in1=CS.t[0:32, 0, 0:n], op=ALU.mult),
                     reads=b1.R() + CS.R(), writes=t1.R())
                S.op("dve", lambda e: e.tensor_tensor(out=t2.t[0:32, 0:n], in0=b2.t[0:32, 0:n], in1=CS.t[0:32, 1, 0:n], op=ALU.mult),
                     reads=b2.R() + CS.R(), writes=t2.R())
                S.op("pool", lambda e: e.tensor_tensor(out=KR.t[:, 0:n], in0=t1.t[0:32, 0:n], in1=t2.t[0:32, 0:n], op=ALU.add),
                     reads=t1.R() + t2.R(), writes=KR.R())
                if need_rest:
                    UP = upb[i % 2]
                    for c in range(2):
                        ba = proj((416 + c * 128, 544 + c * 128))
                        bg = proj((672 + c * 128, 800 + c * 128))
                        S.op("act", lambda e: e.activation(out=sig.t[:, 0:n], in_=bg.t[:, 0:n], func=AF.Sigmoid),
                             reads=bg.R(), writes=sig.R())
                        S.op("dve", lambda e: e.tensor_tensor(out=UP.t[:, c, 0:n], in0=ba.t[:, 0:n], in1=sig.t[:, 0:n], op=ALU.mult),
                             reads=ba.R() + sig.R(), writes=UP.R(c))
                    for c in range(2):
                        bb = proj((928 + c * 128, 1056 + c * 128))
                        bc = proj((1184 + c * 128, 1312 + c * 128))
                        bh = proj((1440 + c * 128, 1568 + c * 128))
                        S.op("act", lambda e: e.activation(out=csb.t[:, 0:n], in_=bc.t[:, 0:n], func=AF.Copy),
                             reads=bc.R(), writes=csb.R())
                        S.op("dve", lambda e: e.tensor_tensor(out=UP.t[:, 2 + c, 0:n], in0=bh.t[:, 0:n], in1=csb.t[:, 0:n], op=ALU.mult),
                             reads=bh.R() + csb.R(), writes=UP.R(2 + c))
                        S.op("act", lambda e: e.activation(out=UP.t[:, 4 + c, 0:n], in_=bb.t[:, 0:n], func=AF.Copy),
                             reads=bb.R(), writes=UP.R(4 + c))
                    cc0 = conv_col(t0, s)

                    def store_up(e):
                        ins = [e.dma_start(out=convT[:, cc0:cc0 + n].rearrange("(c p) t -> p c t", p=128), in_=UP.t[:, :, 0:n])]
                        if s == 0 and t0 == 0:
                            ins.append(e.dma_start(out=send_edge[:, 0:HALO].rearrange("(c p) t -> p c t", p=128),
                                                   in_=UP.t[:, 0:4, 0:HALO]))
                        if s == 0 and t0 + n == OWN:
                            ins.append(e.dma_start(out=send_edge[:, HALO:2 * HALO].rearrange("(c p) t -> p c t", p=128),
                                                   in_=UP.t[:, 0:4, n - HALO:n]))
                        return ins
                    nd = 1 + int(s == 0 and t0 == 0) + int(s == 0 and t0 + n == OWN)
                    S.op("sp", store_up, reads=UP.R(), writes=[R_conv, R_sedge], chan=ch_up[i % 2], ndma=nd)
                bq = acc_bank()
                for c in range(2):
                    S.op("pe", lambda e: e.matmul(bq.t[:, 0:n], lhsT=onesr.t[:], rhs=zqs.t[:, c, 0:n], start=(c == 0), stop=(c == 1)),
                         reads=onesr.R() + zqs.R(c), writes=bq.R())
                bkv = acc_bank()
                S.op("pe", lambda e: e.matmul(bkv.t[:, 0:n], lhsT=onesr.t[:], rhs=ckvs.t[:, 0:n], start=True, stop=True),
                     reads=onesr.R() + ckvs.R(), writes=bkv.R())
                emit_rstd(bq, rs_t, rsq, 128, n, 256.0)
                emit_rstd(bkv, rs_t2, rskv, 128, n, 128.0)
                for c in range(2):
                    S.op("dve", lambda e: e.scalar_tensor_tensor(
                        out=zqn.t[:, c, 0:n], in0=zq.t[:, c, 0:n], scalar=vec[l].t[:, V_GQ + c:V_GQ + c + 1],
                        in1=rsq.t[:, 0:n], op0=ALU.mult, op1=ALU.mult), reads=zq.R(c) + rsq.R() + vec[l].R(), writes=zqn.R(c))
                L = lat[i % 2]
                S.op("dve", lambda e: e.scalar_tensor_tensor(
                    out=L.t[:, 0:n], in0=ckv.t[:, 0:n], scalar=vec[l].t[:, V_GKV:V_GKV + 1], in1=rskv.t[:, 0:n],
                    op0=ALU.mult, op1=ALU.mult), reads=ckv.R() + rskv.R() + vec[l].R(), writes=L.R())
                if s == 0:
                    S.op("sp", lambda e: [e.dma_start(out=send_lat[0:128, t0:t0 + n], in_=L.t[:, 0:n]),
                                          e.dma_start(out=send_lat[128:160, t0:t0 + n], in_=KR.t[:, 0:n])],
                         reads=L.R() + KR.R(), writes=[R_send], chan=ch_lat[i % 2], ndma=2)
                else:
                    S.op("sp", lambda e: [e.dma_start(out=lat_ctx[0:128, t0 - OWN:t0 - OWN + n], in_=L.t[:, 0:n]),
                                          e.dma_start(out=lat_ctx[128:160, t0 - OWN:t0 - OWN + n], in_=KR.t[:, 0:n])],
                         reads=L.R() + KR.R(), writes=[R_latctx], chan=ch_lat[i % 2], ndma=2)
                if i + 1 < len(cl):
                    norm(i + 1)
                if need_rest:
                    QN, QR = qn[i % 2], qr[i % 2]

                    def qproj(wt, g, m=128):
                        b = next_bank()
                        for c in range(2):
                            S.op("pe", lambda e: e.matmul(b.t[0:m, 0:n], lhsT=wt.t[:, c, g * 128:(g + 1) * 128], rhs=zqn.t[:, c, 0:n],
                                                          start=(c == 0), stop=(c == 1)), reads=wt.R() + zqn.R(c), writes=b.R())
                        return b
                    for g in range(4):
                        b = qproj(wqn, g)
                        S.op("act", lambda e: e.activation(out=QN.t[:, g, 0:n], in_=b.t[:, 0:n], func=AF.Copy),
                             reads=b.R(), writes=QN.R(g))
                    for g in range(2):
                        b1 = qproj(wqr, g)
                        b2 = qproj(wqp, g)
                        S.op("dve", lambda e: e.tensor_tensor(out=t1.t[:, 0:n], in0=b1.t[:, 0:n], in1=CS.t[:, 0, 0:n], op=ALU.mult),
                             reads=b1.R() + CS.R(), writes=t1.R())
                        S.op("dve", lambda e: e.tensor_tensor(out=t2.t[:, 0:n], in0=b2.t[:, 0:n], in1=CS.t[:, 1, 0:n], op=ALU.mult),
                             reads=b2.R() + CS.R(), writes=t2.R())
                        S.op("pool", lambda e: e.tensor_tensor(out=QR.t[:, g, 0:n], in0=t1.t[:, 0:n], in1=t2.t[:, 0:n], op=ALU.add),
                             reads=t1.R() + t2.R(), writes=QR.R(g))

                    def store_q(e):
                        ins = []
                        for h in range(HEADS):
                            ins.append(e.dma_start(out=qT[h, 0:64, t0:t0 + n], in_=QN.t[(h % 2) * 64:(h % 2) * 64 + 64, h // 2, 0:n]))
                            ins.append(e.dma_start(out=qT[h, 64:96, t0:t0 + n], in_=QR.t[(h % 4) * 32:(h % 4) * 32 + 32, h // 4, 0:n]))
                        return ins
                    S.op("sp", store_q, reads=QN.R() + QR.R(), writes=[R_q], chan=ch_q[i % 2], ndma=16)

            load_x(0)
            if len(cl) > 1:
                load_x(1)
            norm(0)
            for i in range(len(cl)):
                body(i)
                if i + 2 < len(cl):
                    load_x(i + 2)
        S.barrier()

    def phase_X(l):
        ch = S.new_chan()

        def cc(e):
            i1 = e.collective_compute("AllGather", ALU.bypass, replica_groups=[[0, 1], [2, 3], [4, 5], [6, 7]],
                                      ins=[send_lat], outs=[recv_lat])
            return i1
        o1 = S.op("pool", cc, reads=[R_send], writes=[R_recv])

        def cc2(e):
            return e.collective_compute("AllGather", ALU.bypass, replica_groups=[[0, 1], [2, 3], [4, 5], [6, 7]],
                                        ins=[send_edge], outs=[recv_edge])
        o2 = S.op("pool", cc2, reads=[R_sedge], writes=[R_redge])
        with ExitStack() as st:
            ed = sb(st, "ed", [128, 4, 2 * HALO], BF16)
            ed2 = sb(st, "ed2", [128, 4, 2 * HALO], BF16)
            S.op("sp", lambda e: [
                e.dma_start(out=ed.t[:, :, 0:HALO],
                            in_=recv_edge[0:512, HALO:2 * HALO].rearrange("(c p) t -> p c t", p=128)),
                e.dma_start(out=ed.t[:, :, HALO:2 * HALO],
                            in_=recv_edge[512:1024, 0:HALO].rearrange("(c p) t -> p c t", p=128))],
                reads=[R_redge], writes=ed.R(), chan=ch, ndma=2)
            S.op("dve", lambda e: e.tensor_scalar(out=ed2.t[:, :, 0:HALO], in0=ed.t[:, :, 0:HALO],
                                                  scalar1=eye.t[:, 128:129], scalar2=None, op0=ALU.mult),
                 reads=ed.R() + eye.R(), writes=ed2.R())
            S.op("dve", lambda e: e.tensor_scalar(out=ed2.t[:, :, HALO:2 * HALO], in0=ed.t[:, :, HALO:2 * HALO],
                                                  scalar1=eye.t[:, 129:130], scalar2=None, op0=ALU.mult),
                 reads=ed.R() + eye.R(), writes=ed2.R())
            ch2 = S.new_chan()

            def st_halo(e):
                ins = [
                    e.dma_start(out=convT[0:512, 0:HALO].rearrange("(c p) t -> p c t", p=128), in_=ed2.t[:, :, 0:HALO]),
                    e.dma_start(out=convT[0:512, LAT0 + OWN:LAT0 + OWN + HALO].rearrange("(c p) t -> p c t", p=128),
                                in_=ed2.t[:, :, HALO:2 * HALO])]
                for c in range(4):
                    ins.append(e.dma_start(out=convT[c * 128:(c + 1) * 128, CTX0 - HALO:CTX0], in_=zero15.t[:]))
                    ins.append(e.dma_start(out=convT[c * 128:(c + 1) * 128, CTX0 + CTX:CTX0 + CTX + HALO], in_=zero15.t[:]))
                return ins
            S.op("sp", st_halo, reads=ed2.R() + zero15.R(), writes=[R_conv], chan=ch2, ndma=10)
            S.barrier()
            if debug_ext:
                b0 = l * 8
                dbg_copy(dbgb[b0 + 0, 0:96, :], qT[3, :, 1024:1280], [R_q])
                dbg_copy(dbgb[b0 + 1, 0:96, :], qT[5, :, OWN:OWN + 256], [R_q])
                dbg_copy(dbgb[b0 + 2, :, :], recv_lat[0:128, 512:768], [R_recv])
                dbg_copy(dbgb[b0 + 3, 0:32, :], recv_lat[288:320, 0:256], [R_recv])
                dbg_copy(dbgb[b0 + 4, :, :], lat_ctx[0:128, :], [R_latctx])
                dbg_copy(dbgb[b0 + 5, :, :], convT[0:128, 0:256], [R_conv])
                dbg_copy(dbgb[b0 + 6, :, :], convT[256:384, LAT0 + OWN + HALO - 256:LAT0 + OWN + HALO], [R_conv])
                dbg_copy(dbgb[b0 + 7, :, :], convT[512:640, CTX0:CTX0 + 256], [R_conv])

    def phase_ATT(l, prefetch=None):
        with ExitStack() as st:
            latT = sb(st, "latT", [128, NKEY], BF16)
            wkv = sb(st, "wkv", [128, 1024], BF16)
            KT = [sb(st, "KT%d" % i, [96, NKEY], BF16, 2) for i in range(2)]
            VH = [sb(st, "VH%d" % i, [128, NKT, 65], BF16, 2) for i in range(2)]
            QH = [sb(st, "QH%d" % i, [96, NTOK], BF16) for i in range(2)]
            PT = [sb(st, "PT%d" % i, [128, 512], BF16) for i in range(4)]
            osb = [sb(st, "osb%d" % i, [65, 512], F32) for i in range(2)]
            rden = [sb(st, "rden%d" % i, [64, 512], F32) for i in range(2)]
            att = [sb(st, "att%d" % i, [64, 512], F32) for i in range(2)]
            ch = S.new_chan()
            S.op("sp", lambda e: [
                e.dma_start(out=latT.t[:, 0:CTX], in_=lat_ctx[0:128, :]),
                e.dma_start(out=latT.t[:, CTX:CTX + OWN], in_=recv_lat[0:128, :]),
                e.dma_start(out=latT.t[:, CTX + OWN:NKEY], in_=recv_lat[160:288, :])],
                reads=[R_latctx, R_recv], writes=latT.R(), chan=ch, ndma=3)
            for i in range(2):
                ch = S.new_chan()
                S.op("sp", lambda e, i=i: [
                    e.dma_start(out=KT[i].t[64:96, 0:CTX], in_=lat_ctx[128:160, :]),
                    e.dma_start(out=KT[i].t[64:96, CTX:CTX + OWN], in_=recv_lat[128:160, :]),
                    e.dma_start(out=KT[i].t[64:96, CTX + OWN:NKEY], in_=recv_lat[288:320, :])],
                    reads=[R_latctx, R_recv], writes=KT[i].R(1), chan=ch, ndma=3)
                S.op("pool", lambda e, i=i: e.memset(VH[i].t[:, :, 64:65], 1.0), writes=VH[i].R(1))
            ch = S.new_chan()
            S.op("pool", lambda e: [e.dma_start(out=wkv.t[:], in_=w_kv_b[l])], writes=wkv.R(), chan=ch, ndma=1)
            if prefetch is not None:
                prefetch()
            chq = [S.new_chan(), S.new_chan()]
            cho = [S.new_chan(), S.new_chan()]
            qchunks = chunks(512, l != DEPTH - 1)
            pt_i = 0
            ob_i = 0
            pend = []
            kvbank = banks[7]
            obanks = banks[5:7]

            def kv_thunks(h, rotate=False):
                K, V = KT[h % 2], VH[h % 2]
                th = []
                for j in range((NKEY + 511) // 512):
                    def f(j=j):
                        kvbank = next_bank() if rotate else banks[7]
                        c0 = j * 512
                        w = min(512, NKEY - c0)
                        S.op("pe", lambda e: e.matmul(kvbank.t[0:64, 0:w], lhsT=wkv.t[:, h * 128:h * 128 + 64],
                                                      rhs=latT.t[:, c0:c0 + w], start=True, stop=True),
                             reads=wkv.R() + latT.R(), writes=kvbank.R())
                        S.op("dve", lambda e: e.tensor_copy(out=K.t[0:64, c0:c0 + w], in_=kvbank.t[0:64, 0:w]),
                             reads=kvbank.R(), writes=K.R(0))
                    th.append(f)
                for g0 in range(0, NKT, 8):
                    def f(g0=g0):
                        kvbank = next_bank() if rotate else banks[7]
                        g1 = min(g0 + 8, NKT)
                        for kt in range(g0, g1):
                            S.op("pe", lambda e: e.matmul(kvbank.t[:, (kt - g0) * 64:(kt - g0 + 1) * 64],
                                                          lhsT=latT.t[:, kt * 128:(kt + 1) * 128],
                                                          rhs=wkv.t[:, h * 128 + 64:h * 128 + 128], start=True, stop=True),
                                 reads=wkv.R() + latT.R(), writes=kvbank.R())
                        S.op("dve", lambda e: e.tensor_copy(
                            out=V.t[:, g0:g1, 0:64], in_=kvbank.t[:, 0:(g1 - g0) * 64].rearrange("p (k d) -> p k d", d=64)),
                            reads=kvbank.R(), writes=V.R(0))
                    th.append(f)
                return th

            for f in kv_thunks(0, rotate=True):
                f()
            pend_kv = []
            it_n = 0
            for h in range(HEADS):
                K = KT[h % 2]
                V = VH[h % 2]
                Q = QH[h % 2]
                if h == 0:
                    S.op("sp", lambda e: [e.dma_start(out=Q.t[:], in_=qT[0])], reads=[R_q], writes=Q.R(),
                         chan=chq[0], ndma=1)
                if h + 1 < HEADS:
                    Qn = QH[(h + 1) % 2]
                    S.op("sp", lambda e: [e.dma_start(out=Qn.t[:], in_=qT[h + 1])], reads=[R_q], writes=Qn.R(),
                         chan=chq[(h + 1) % 2], ndma=1)
                while pend_kv:
                    pend_kv.pop(0)()
                if h + 1 < HEADS:
                    pend_kv = kv_thunks(h + 1)
                for (t0, n, s) in qchunks:
                    kts = list(range(NKT)) if s == 0 else [0, 1]
                    ob = obanks[ob_i % 2]
                    sbanks = {}

                    def qk(kt):
                        b = next_bank()
                        sbanks[kt] = b
                        S.op("pe", lambda e, b=b, kt=kt: e.matmul(
                            b.t[:, 0:n], lhsT=K.t[0:96, kt * 128:(kt + 1) * 128], rhs=Q.t[0:96, t0:t0 + n],
                            start=True, stop=True), reads=K.R() + Q.R(), writes=b.R())
                    LOOK = 3
                    for a in range(min(LOOK, len(kts))):
                        qk(kts[a])
                    for ai, kt in enumerate(kts):
                        b = sbanks.pop(kt)
                        P = PT[pt_i % 4]
                        pt_i += 1
                        S.op("act", lambda e, b=b, P=P: e.activation(out=P.t[:, 0:n], in_=b.t[:, 0:n], func=AF.Exp,
                                                                      scale=SM_SCALE), reads=b.R(), writes=P.R())
                        if ai + LOOK < len(kts):
                            qk(kts[ai + LOOK])
                        if ai == 1 and pend:
                            pend.pop(0)()
                        it_n += 1
                        if it_n % 6 == 0 and pend_kv:
                            pend_kv.pop(0)()
                        S.op("pe", lambda e, P=P, kt=kt, ai=ai: e.matmul(
                            ob.t[0:65, 0:n], lhsT=V.t[:, kt, 0:65], rhs=P.t[:, 0:n],
                            start=(ai == 0), stop=(ai == len(kts) - 1)), reads=V.R() + P.R(), writes=ob.R())
                    OS = osb[ob_i % 2]
                    RD = rden[ob_i % 2]
                    AT = att[ob_i % 2]
                    ob_i += 1
                    S.op("dve", lambda e: e.tensor_copy(out=OS.t[:, 0:n], in_=ob.t[0:65, 0:n]), reads=ob.R(), writes=OS.R())

                    def tail(OS=OS, RD=RD, AT=AT, n=n, t0=t0, h=h, oi=ob_i):
                        bd = next_bank()
                        S.op("pe", lambda e: e.matmul(bd.t[0:64, 0:n], lhsT=sel.t[:], rhs=OS.t[:, 0:n], start=True, stop=True),
                             reads=sel.R() + OS.R(), writes=bd.R())
                        S.op("dve", lambda e: e.reciprocal(out=RD.t[:, 0:n], in_=bd.t[0:64, 0:n]), reads=bd.R(), writes=RD.R())
                        S.op("pool", lambda e: e.tensor_tensor(out=AT.t[:, 0:n], in0=OS.t[0:64, 0:n], in1=RD.t[:, 0:n], op=ALU.mult),
                             reads=OS.R() + RD.R(), writes=AT.R())
                        S.op("sp", lambda e: [e.dma_start(out=attnT[h, :, t0:t0 + n], in_=AT.t[:, 0:n])],
                             reads=AT.R(), writes=[R_attn], chan=cho[oi % 2], ndma=1)
                    pend.append(tail)
            while pend:
                pend.pop(0)()
        S.barrier()
        if debug_ext:
            dbg_copy(dbgf[l * 6 + 0, 0:64, :], attnT[2, :, 512:768], [R_attn])
            dbg_copy(dbgf[l * 6 + 1, 0:64, :], attnT[7, :, OWN:OWN + 256], [R_attn])

    def alloc_c1w(stack):
        return (sb(stack, "woa", [64, 8, 1024], BF16), sb(stack, "wob", [128, 4, 1024], BF16),
                sb(stack, "dgc", [128, 62, 128], BF16), sb(stack, "dgs", [128, 6, 128], BF16))

    c1w_chan = S.new_chan()

    def load_c1w(l, c1w):
        woa, wob, dgc, dgs = c1w
        S.op("pool", lambda e: [e.dma_start(out=woa.t[:], in_=w_o[l, 0:512, :].rearrange("(h r) n -> r h n", r=64)),
                                e.dma_start(out=wob.t[:], in_=w_o[l, 512:1024, :].rearrange("(c p) n -> p c n", p=128))],
             writes=woa.R() + wob.R(), chan=c1w_chan, ndma=2)
        for c in range(2):
            for k in range(31):
                S.op("pool", lambda e: e.tensor_scalar(
                    out=dgc.t[:, c * 31 + k, :], in0=eyeb.t[:], scalar1=vec[l].t[:, V_CW + c * 31 + k:V_CW + c * 31 + k + 1],
                    scalar2=None, op0=ALU.mult), reads=eyeb.R() + vec[l].R(), writes=dgc.R())
            for k in range(3):
                S.op("pool", lambda e: e.tensor_scalar(
                    out=dgs.t[:, c * 3 + k, :], in0=eyeb.t[:], scalar1=vec[l].t[:, V_SW + c * 3 + k:V_SW + c * 3 + k + 1],
                    scalar2=None, op0=ALU.mult), reads=eyeb.R() + vec[l].R(), writes=dgs.R())

    def phase_C1(l, xsrc, R_x, c1w):
        NC1 = 256
        n = NC1
        with ExitStack() as st:
            woa, wob, dgc, dgs = c1w

            class B:
                pass
            bufs = []
            for si in range(2):
                b = B()
                b.xc = sb(st, "xc1", [128, 8, n], F32, 8)
                b.at = sb(st, "at", [64, 8, n], F32)
                b.uw = sb(st, "uw", [128, 4, n + 2 * HALO], BF16)
                b.bw = sb(st, "bw", [128, 2, n], BF16)
                b.sqa = sb(st, "sqa", [64, 2, n], F32, 2)
                b.an = sb(st, "an", [64, 8, n], BF16, 8)
                b.tmp = sb(st, "c1tmp", [128, n], F32)
                b.rs = sb(st, "c1rs", [128, n], F32)
                b.v = sb(st, "c1v", [128, 2, n], F32, 2)
                b.vs = sb(st, "c1vs", [128, 2, n], F32R, 2)
                b.mean = sb(st, "c1mean", [128, n], F32)
                b.msq = sb(st, "c1msq", [128, n], F32)
                b.var = sb(st, "c1var", [128, n], F32)
                b.dd = sb(st, "c1d", [128, 2, n], F32, 2)
                b.cn = sb(st, "cn", [128, 2, n], BF16, 2)
                b.sn = sb(st, "sn", [128, 2, n], BF16, 2)
                b.y = sb(st, "c1y", [128, 8, n], F32, 8)
                b.ys = sb(st, "c1ys", [128, 3, n], F32R, 3)
                b.sd = sb(st, "c1sd", [128, 2, n], F32, 2)
                b.svs = sb(st, "c1svs", [128, 2, n], F32R, 2)
                b.sq4 = sb(st, "c1sq4", [64, 4, n], F32R, 4)
                b.rsL = sb(st, "c1rsL", [128, n], F32)
                b.rsS = sb(st, "c1rsS", [128, n], F32)
                b.rsA = sb(st, "c1rsA", [128, n], F32)
                b.tmpS = sb(st, "c1tmpS", [128, n], F32)
                b.tmpA = sb(st, "c1tmpA", [128, n], F32)
                b.rot = banks[si * 4:si * 4 + 2]
                b.acc = banks[si * 4 + 2:si * 4 + 4]
                b.ri = 0
                b.ai = 0
                b.chl = [S.new_chan() for _ in range(4)]
                b.chs = S.new_chan()
                bufs.append(b)

            xv = xsrc.rearrange("(c p) t -> p c t", p=128)
            xo = xmid.rearrange("(c p) t -> p c t", p=128)
            cl = chunks(NC1, l != DEPTH - 1)

            def gen(i, b):
                t0, _, s = cl[i]
                cc0 = conv_col(t0, s)
                X, A, U, Bw = b.xc, b.at, b.uw, b.bw
                tmp, rs, v, vs, mean, msq, var, dd, cn, sn, y, ys, sqa, an = (
                    b.tmp, b.rs, b.v, b.vs, b.mean, b.msq, b.var, b.dd, b.cn, b.sn, b.y, b.ys, b.sqa, b.an)

                def rot():
                    k = b.rot[b.ri % 2]
                    b.ri += 1
                    return k

                def acc():
                    k = b.acc[b.ai % 2]
                    b.ai += 1
                    return k
                S.op("sp", lambda e: [e.dma_start(out=U.t[:, :, 0:n + 2 * HALO],
                                                  in_=convT[0:512, cc0 - HALO:cc0 + n + HALO].rearrange("(c p) t -> p c t", p=128))],
                     reads=[R_conv], writes=U.R(), chan=b.chl[2], ndma=1)
                S.op("sp", lambda e: [e.dma_start(out=Bw.t[:, :, 0:n],
                                                  in_=convT[512:768, cc0:cc0 + n].rearrange("(c p) t -> p c t", p=128))],
                     reads=[R_conv], writes=Bw.R(), chan=b.chl[3], ndma=1)
                S.op("sp", lambda e: [e.dma_start(out=A.t[:, :, 0:n], in_=attnT[:, :, t0:t0 + n].rearrange("h r t -> r h t"))],
                     reads=[R_attn], writes=A.R(), chan=b.chl[1], ndma=1)
                S.op("sp", lambda e: [e.dma_start(out=X.t[:, :, 0:n], in_=xv[:, :, t0:t0 + n])],
                     reads=[R_x], writes=X.R(), chan=b.chl[0], ndma=1)
                yield
                sd, svs, rsL, rsS, rsA, tmpS, tmpA = b.sd, b.svs, b.rsL, b.rsS, b.rsA, b.tmpS, b.tmpA
                for c in range(2):
                    bc = rot()
                    for k in range(31):
                        S.op("pe", lambda e, k=k: e.matmul(bc.t[:, 0:n], lhsT=dgc.t[:, c * 31 + k, :], rhs=U.t[:, c, k:k + n],
                                                           start=(k == 0), stop=(k == 30)), reads=dgc.R() + U.R(), writes=bc.R())
                    S.op("dve", lambda e: e.tensor_scalar(out=v.t[:, c, 0:n], in0=bc.t[:, 0:n],
                                                          scalar1=vec[l].t[:, V_DWB + c:V_DWB + c + 1], scalar2=None, op0=ALU.add),
                         reads=bc.R() + vec[l].R(), writes=v.R(c))
                    S.op("act", lambda e: e.activation(out=vs.t[:, c, 0:n], in_=v.t[:, c, 0:n], func=AF.Square),
                         reads=v.R(c), writes=vs.R(c))
                    yield
                for c in range(2):
                    bc = rot()
                    for k in range(3):
                        S.op("pe", lambda e, k=k: e.matmul(bc.t[:, 0:n], lhsT=dgs.t[:, c * 3 + k, :],
                                                           rhs=U.t[:, 2 + c, HALO - 1 + k:HALO - 1 + k + n],
                                                           start=(k == 0), stop=(k == 2)), reads=dgs.R() + U.R(), writes=bc.R())
                    S.op("dve", lambda e: e.tensor_tensor(out=sd.t[:, c, 0:n], in0=bc.t[:, 0:n], in1=Bw.t[:, c, 0:n],
                                                          op=ALU.mult), reads=bc.R() + Bw.R(), writes=sd.R(c))
                    S.op("pool", lambda e: e.tensor_tensor(out=svs.t[:, c, 0:n], in0=sd.t[:, c, 0:n], in1=sd.t[:, c, 0:n],
                                                           op=ALU.mult), reads=sd.R(c), writes=svs.R(c))
                sq4 = b.sq4
                for h in range(4):
                    S.op("pool" if h % 2 else "act", (lambda e: e.tensor_tensor(
                        out=sq4.t[:, h, 0:n], in0=A.t[:, h, 0:n], in1=A.t[:, h, 0:n], op=ALU.mult)) if h % 2 else
                        (lambda e: e.activation(out=sq4.t[:, h, 0:n], in_=A.t[:, h, 0:n], func=AF.Square)),
                        reads=A.R(), writes=sq4.R(h))
                yield
                b1 = acc()
                b2 = acc()
                for c in range(2):
                    S.op("pe", lambda e: e.matmul(b1.t[:, 0:n], lhsT=ones.t[:], rhs=v.t[:, c, 0:n],
                                                  start=(c == 0), stop=(c == 1)), reads=ones.R() + v.R(c), writes=b1.R())
                for c in range(2):
                    S.op("pe", lambda e: e.matmul(b2.t[:, 0:n], lhsT=onesr.t[:], rhs=vs.t[:, c, 0:n],
                                                  start=(c == 0), stop=(c == 1)), reads=onesr.R() + vs.R(c), writes=b2.R())
                S.op("dve", lambda e: e.tensor_scalar(out=mean.t[:, 0:n], in0=b1.t[:, 0:n], scalar1=1.0 / 256, scalar2=None,
                                                      op0=ALU.mult), reads=b1.R(), writes=mean.R())
                S.op("dve", lambda e: e.tensor_tensor(out=msq.t[:, 0:n], in0=mean.t[:, 0:n], in1=mean.t[:, 0:n], op=ALU.mult),
                     reads=mean.R(), writes=msq.R())
                S.op("dve", lambda e: e.scalar_tensor_tensor(out=var.t[:, 0:n], in0=b2.t[:, 0:n], scalar=1.0 / 256,
                                                             in1=msq.t[:, 0:n], op0=ALU.mult, op1=ALU.subtract),
                     reads=b2.R() + msq.R(), writes=var.R())
                S.op("act", lambda e: e.activation(out=tmp.t[:, 0:n], in_=var.t[:, 0:n], func=AF.Sqrt, bias=epsb.t[:, 0:1],
                                                   scale=1.0), reads=var.R() + epsb.R(), writes=tmp.R())
                S.op("dve", lambda e: e.reciprocal(out=rsL.t[:, 0:n], in_=tmp.t[:, 0:n]), reads=tmp.R(), writes=rsL.R())
                bsa = acc()
                for h in range(4):
                    S.op("pe", lambda e: e.matmul(bsa.t[:, 0:n], lhsT=onesr.t[0:64, :], rhs=sq4.t[:, h, 0:n],
                                                  start=(h == 0), stop=False), reads=onesr.R() + sq4.R(h), writes=bsa.R())
                for h in range(4, 8):
                    S.op("pool" if h % 2 else "act", (lambda e: e.tensor_tensor(
                        out=sq4.t[:, h % 4, 0:n], in0=A.t[:, h, 0:n], in1=A.t[:, h, 0:n], op=ALU.mult)) if h % 2 else
                        (lambda e: e.activation(out=sq4.t[:, h % 4, 0:n], in_=A.t[:, h, 0:n], func=AF.Square)),
                        reads=A.R(), writes=sq4.R(h % 4))
                yield
                b4 = acc()
                for c in range(2):
                    S.op("pe", lambda e: e.matmul(b4.t[:, 0:n], lhsT=onesr.t[:], rhs=svs.t[:, c, 0:n],
                                                  start=(c == 0), stop=(c == 1)), reads=onesr.R() + svs.R(c), writes=b4.R())
                for c in range(2):
                    S.op("dve", lambda e: e.tensor_tensor(out=dd.t[:, c, 0:n], in0=v.t[:, c, 0:n], in1=mean.t[:, 0:n],
                                                          op=ALU.subtract), reads=v.R(c) + mean.R(), writes=dd.R(c))
                    S.op("dve", lambda e: e.scalar_tensor_tensor(
                        out=dd.t[:, c, 0:n], in0=dd.t[:, c, 0:n], scalar=vec[l].t[:, V_LNG + c:V_LNG + c + 1],
                        in1=rsL.t[:, 0:n], op0=ALU.mult, op1=ALU.mult), reads=dd.R(c) + rsL.R() + vec[l].R(), writes=dd.R(c))
                    S.op("act", lambda e: e.activation(out=v.t[:, c, 0:n], in_=dd.t[:, c, 0:n], func=AF.Silu,
                                                       bias=vec[l].t[:, V_LNB + c:V_LNB + c + 1], scale=1.0),
                         reads=dd.R(c) + vec[l].R(), writes=v.R(c))
                    S.op("act", lambda e: e.activation(out=vs.t[:, c, 0:n], in_=v.t[:, c, 0:n], func=AF.Square),
                         reads=v.R(c), writes=vs.R(c))
                emit_rstd(b4, tmpS, rsS, 128, n, 256.0)
                for c in range(2):
                    S.op("dve", lambda e: e.scalar_tensor_tensor(
                        out=sn.t[:, c, 0:n], in0=sd.t[:, c, 0:n], scalar=vec[l].t[:, V_GBS + c:V_GBS + c + 1],
                        in1=rsS.t[:, 0:n], op0=ALU.mult, op1=ALU.mult), reads=sd.R(c) + rsS.R() + vec[l].R(), writes=sn.R(c))
                for h in range(4, 8):
                    S.op("pe", lambda e: e.matmul(bsa.t[:, 0:n], lhsT=onesr.t[0:64, :], rhs=sq4.t[:, h % 4, 0:n],
                                                  start=False, stop=(h == 7)), reads=onesr.R() + sq4.R(h % 4), writes=bsa.R())
                emit_rstd(bsa, tmpA, rsA, 64, n, 512.0)
                for h in range(8):
                    S.op("dve", lambda e: e.scalar_tensor_tensor(
                        out=an.t[:, h, 0:n], in0=A.t[:, h, 0:n], scalar=vec[l].t[0:64, V_GBA + h:V_GBA + h + 1],
                        in1=rsA.t[0:64, 0:n], op0=ALU.mult, op1=ALU.mult), reads=A.R() + rsA.R() + vec[l].R(), writes=an.R(h))
                yield
                b3 = acc()
                for c in range(2):
                    S.op("pe", lambda e: e.matmul(b3.t[:, 0:n], lhsT=onesr.t[:], rhs=vs.t[:, c, 0:n],
                                                  start=(c == 0), stop=(c == 1)), reads=onesr.R() + vs.R(c), writes=b3.R())
                emit_rstd(b3, tmp, rs, 128, n, 256.0)
                for c in range(2):
                    S.op("dve", lambda e: e.scalar_tensor_tensor(
                        out=cn.t[:, c, 0:n], in0=v.t[:, c, 0:n], scalar=vec[l].t[:, V_GBC + c:V_GBC + c + 1],
                        in1=rs.t[:, 0:n], op0=ALU.mult, op1=ALU.mult), reads=v.R(c) + rs.R() + vec[l].R(), writes=cn.R(c))
                yield
                bsy = acc()

                def ssy(o2):
                    S.op("pe", lambda e: e.matmul(bsy.t[:, 0:n], lhsT=onesr.t[:], rhs=ys.t[:, o2 % 3, 0:n],
                                                  start=(o2 == 0), stop=(o2 == 7)), reads=onesr.R() + ys.R(o2 % 3), writes=bsy.R())
                for oc in range(8):
                    bo = rot()
                    for h in range(8):
                        S.op("pe", lambda e: e.matmul(bo.t[:, 0:n], lhsT=woa.t[:, h, oc * 128:(oc + 1) * 128], rhs=an.t[:, h, 0:n],
                                                      start=(h == 0), stop=False), reads=woa.R() + an.R(h), writes=bo.R())
                    for c in range(2):
                        S.op("pe", lambda e: e.matmul(bo.t[:, 0:n], lhsT=wob.t[:, 2 + c, oc * 128:(oc + 1) * 128], rhs=sn.t[:, c, 0:n],
                                                      start=False, stop=False), reads=wob.R() + sn.R(c), writes=bo.R())
                    for c in range(2):
                        S.op("pe", lambda e: e.matmul(bo.t[:, 0:n], lhsT=wob.t[:, c, oc * 128:(oc + 1) * 128], rhs=cn.t[:, c, 0:n],
                                                      start=False, stop=(c == 1)), reads=wob.R() + cn.R(c), writes=bo.R())
                    if oc >= 2:
                        ssy(oc - 2)
                    S.op("act", lambda e: e.activation(out=y.t[:, oc, 0:n], in_=bo.t[:, 0:n], func=AF.Copy),
                         reads=bo.R(), writes=y.R(oc))
                    S.op("pool" if oc % 2 else "act", (lambda e: e.tensor_tensor(
                        out=ys.t[:, oc % 3, 0:n], in0=y.t[:, oc, 0:n], in1=y.t[:, oc, 0:n], op=ALU.mult)) if oc % 2 else
                        (lambda e: e.activation(out=ys.t[:, oc % 3, 0:n], in_=y.t[:, oc, 0:n], func=AF.Square)),
                        reads=y.R(oc), writes=ys.R(oc % 3))
                    if oc % 4 == 3:
                        yield
                ssy(6)
                ssy(7)
                emit_rstd(bsy, tmp, rs, 128, n, float(D))
                for oc in range(8):
                    S.op("dve", lambda e: e.scalar_tensor_tensor(
                        out=y.t[:, oc, 0:n], in0=y.t[:, oc, 0:n], scalar=mod_ap(l, 2, oc, s), in1=rs.t[:, 0:n],
                        op0=ALU.mult, op1=ALU.mult), reads=y.R(oc) + rs.R() + MOD[l].R(), writes=y.R(oc))
                    S.op("pool", lambda e: e.tensor_tensor(out=X.t[:, oc, 0:n], in0=X.t[:, oc, 0:n], in1=y.t[:, oc, 0:n],
                                                           op=ALU.add), reads=X.R(oc) + y.R(oc), writes=X.R(oc))
                S.op("pool", lambda e: [e.dma_start(out=xo[:, :, t0:t0 + n], in_=X.t[:, :, 0:n])],
                     reads=X.R(), writes=[R_xmid], chan=b.chs, ndma=1)

            active = []
            nxt = 0
            while active or nxt < len(cl):
                while len(active) < 2 and nxt < len(cl):
                    active.append(gen(nxt, bufs[nxt % 2]))
                    nxt += 1
                    if len(active) == 2 and nxt == 2:
                        for _ in range(3):
                            next(active[0])
                for g in list(active):
                    try:
                        next(g)
                    except StopIteration:
                        active.remove(g)
        S.barrier()
        if debug_ext:
            dbg_copy(dbgf[l * 6 + 2, :, :], xmid[128:256, 1024:1280], [R_xmid])
            dbg_copy(dbgf[l * 6 + 3, :, :], xmid[0:128, OWN:OWN + 256], [R_xmid])

    c2_chans = [S.new_chan() for _ in range(8)]

    def phase_C2(l, xdst, R_dst, is_out):
        NC2 = 256
        with ExitStack() as st:
            w1s = sb(st, "w1s", [128, 8, DFF], BF16, 4)
            w2s = sb(st, "w2s", [128, 32, D], BF16, 4)
            xc = [sb(st, "xc2_%d" % i, [128, 8, NC2], F32, 8) for i in range(3)]
            sqP = sb(st, "c2sqP", [128, 3, NC2], F32R, 3)
            sqE = sb(st, "c2sqE", [128, 2, NC2], F32R, 2)
            tmpn = [sb(st, "c2tn%d" % i, [128, NC2], F32) for i in range(2)]
            tme = [sb(st, "c2te%d" % i, [128, NC2], F32) for i in range(2)]
            tmpP = sb(st, "c2tmpP", [128, NC2], F32)
            rsP = sb(st, "c2rsP", [128, NC2], F32)
            tmpE = sb(st, "c2tmpE", [128, NC2], F32)
            rsE = sb(st, "c2rsE", [128, NC2], F32)
            h2 = [sb(st, "h2_%d" % i, [128, 8, NC2], BF16, 8) for i in range(2)]
            rl = [sb(st, "rl%d" % i, [128, NC2], F32) for i in range(2)]
            a2 = sb(st, "a2", [128, 32, NC2], BF16, 32)
            w1v = w1[l].rearrange("(c p) n -> p c n", p=128)
            w2v = w2[l].rearrange("(c p) n -> p c n", p=128)
            for g in range(4):
                S.op("pool", lambda e: [e.dma_start(out=w1s.t[:, :, (2 * g + q) * 512:(2 * g + q + 1) * 512],
                                                    in_=w1v[:, :, (2 * g + q) * 512:(2 * g + q + 1) * 512]) for q in range(2)],
                     writes=w1s.R(g), chan=c2_chans[g], ndma=2)
            for g in range(4):
                S.op("pool", lambda e: [e.dma_start(out=w2s.t[:, (2 * g + q) * 4:(2 * g + q + 1) * 4, :],
                                                    in_=w2v[:, (2 * g + q) * 4:(2 * g + q + 1) * 4, :]) for q in range(2)],
                     writes=w2s.R(g), chan=c2_chans[4 + g], ndma=2)
            xv = xmid.rearrange("(c p) t -> p c t", p=128)
            xo = xdst.rearrange("(c p) t -> p c t", p=128)
            cl = chunks(NC2, l != DEPTH - 1)
            NCH = len(cl)
            chl = [S.new_chan() for _ in range(3)]
            chs = [S.new_chan() for _ in range(3)]
            ybank = banks[0:4]
            hbank = banks[4:6]
            ssPb, ssEb = banks[6], banks[7]
            n = NC2
            bg = []

            def pump(k):
                for _ in range(k):
                    if bg:
                        bg.pop(0)()

            def flush():
                while bg:
                    bg.pop(0)()

            def load(i):
                t0 = cl[i][0]
                X = xc[i % 3]
                S.op("sp", lambda e: [e.dma_start(out=X.t[:, :, 0:n], in_=xv[:, :, t0:t0 + n])],
                     reads=[R_xmid], writes=X.R(), chan=chl[i % 3], ndma=1)

            def prologue(i):
                t0, _, s = cl[i]
                X, H = xc[i % 3], h2[i % 2]
                th = []

                def ssp(c):
                    S.op("pe", lambda e: e.matmul(ssPb.t[:, 0:n], lhsT=onesr.t[:], rhs=sqP.t[:, c % 3, :],
                                                  start=(c == 0), stop=(c == 7)), reads=onesr.R() + sqP.R(c % 3), writes=ssPb.R())
                for c in range(8):
                    def f(c=c):
                        if c >= 2:
                            ssp(c - 2)
                        S.op("pool", lambda e: e.tensor_tensor(out=sqP.t[:, c % 3, :], in0=X.t[:, c, :], in1=X.t[:, c, :],
                                                               op=ALU.mult), reads=X.R(c), writes=sqP.R(c % 3))
                    th.append(f)
                th.append(lambda: (ssp(6), ssp(7)))
                th.append(lambda: emit_rstd(ssPb, tmpP, rsP, 128, n, float(D)))
                for c in range(8):
                    def f(c=c):
                        tm = tmpn[c % 2]
                        S.op("dve", lambda e: e.tensor_tensor(out=tm.t[:, :], in0=X.t[:, c, :], in1=rsP.t[:, :], op=ALU.mult),
                             reads=X.R(c) + rsP.R(), writes=tm.R())
                        S.op("dve", lambda e: e.tensor_scalar(out=H.t[:, c, :], in0=tm.t[:, :], scalar1=mod_ap(l, 3, c, s),
                                                              scalar2=mod_ap(l, 4, c, s), op0=ALU.mult, op1=ALU.add),
                             reads=tm.R() + MOD[l].R(), writes=H.R(c))
                    th.append(f)
                return th

            def epilogue(i):
                t0, _, s = cl[i]
                X = xc[i % 3]
                th = [lambda: emit_rstd(ssEb, tmpE, rsE, 128, n, float(D))]
                for oc in range(8):
                    def f(oc=oc):
                        yb = ybank[oc // 2]
                        c0 = (oc % 2) * n
                        tm = tme[oc % 2]
                        S.op("dve", lambda e: e.scalar_tensor_tensor(
                            out=tm.t[:, :], in0=yb.t[:, c0:c0 + n], scalar=mod_ap(l, 5, oc, s), in1=rsE.t[:, :],
                            op0=ALU.mult, op1=ALU.mult), reads=yb.R() + rsE.R() + MOD[l].R(), writes=tm.R())
                        S.op("pool", lambda e: e.tensor_tensor(out=X.t[:, oc, :], in0=X.t[:, oc, :], in1=tm.t[:, :], op=ALU.add),
                             reads=X.R(oc) + tm.R(), writes=X.R(oc))
                    th.append(f)

                def st_():
                    o = S.op("sp", lambda e: [e.dma_start(out=xo[:, :, t0:t0 + n], in_=X.t[:, :, :])],
                             reads=X.R(), writes=[R_dst], chan=chs[i % 3], ndma=1)
                    if is_out:
                        final_ops.append(o)
                th.append(st_)
                return th

            def main(i):
                H = h2[i % 2]
                for hc in range(32):
                    hb = hbank[hc % 2]
                    for k in range(8):
                        S.op("pe", lambda e, k=k: e.matmul(hb.t[:, 0:n], lhsT=w1s.t[:, k, hc * 128:(hc + 1) * 128],
                                                           rhs=H.t[:, k, :], start=(k == 0), stop=(k == 7)),
                             reads=w1s.R(hc // 8) + H.R(k), writes=hb.R())
                    r = rl[hc % 2]
                    S.op("act", lambda e: e.activation(out=r.t[:, :], in_=hb.t[:, 0:n], func=AF.Relu),
                         reads=hb.R(), writes=r.R())
                    S.op("pool" if hc % 2 else "dve", lambda e: e.tensor_tensor(
                        out=a2.t[:, hc, :], in0=r.t[:, :], in1=r.t[:, :], op=ALU.mult), reads=r.R(), writes=a2.R(hc))
                    pump(1)
                while bg and getattr(bg[0], "_epi", False):
                    bg.pop(0)()

                def ss_mm(oc):
                    S.op("pe", lambda e: e.matmul(ssEb.t[:, 0:n], lhsT=onesr.t[:], rhs=sqE.t[:, oc % 2, :],
                                                  start=(oc == 0), stop=(oc == 7)), reads=onesr.R() + sqE.R(oc % 2), writes=ssEb.R())
                for oc in range(8):
                    yb = ybank[oc // 2]
                    c0 = (oc % 2) * n
                    for hc in range(32):
                        S.op("pe", lambda e, hc=hc: e.matmul(yb.t[:, c0:c0 + n], lhsT=w2s.t[:, hc, oc * 128:(oc + 1) * 128],
                                                             rhs=a2.t[:, hc, :], start=(hc == 0), stop=(hc == 31)),
                             reads=w2s.R(hc // 8) + a2.R(hc), writes=yb.R())
                    if oc % 2 == 1:
                        if oc >= 3:
                            ss_mm(oc - 3)
                            ss_mm(oc - 2)
                        for o2 in (oc - 1, oc):
                            cc = (o2 % 2) * n
                            S.op("act", lambda e, o2=o2, cc=cc: e.activation(out=sqE.t[:, o2 % 2, :], in_=yb.t[:, cc:cc + n],
                                                                             func=AF.Square), reads=yb.R(), writes=sqE.R(o2 % 2))
                    pump(2)
                ss_mm(6)
                ss_mm(7)
                flush()

            load(0)
            if NCH > 1:
                load(1)
            for f in prologue(0):
                f()
            for i in range(NCH):
                if i + 1 < NCH:
                    bg.extend(prologue(i + 1))
                main(i)
                ep = epilogue(i)
                for f in ep:
                    pass
                marked = []
                for f in ep:
                    def g(f=f):
                        f()
                    g._epi = True
                    marked.append(g)
                bg[0:0] = marked
                if i + 2 < NCH:
                    load(i + 2)
            flush()
        S.barrier()
        if debug_ext:
            dbg_copy(dbgf[l * 6 + 4, :, :], xdst[256:384, 2048:2304], [R_dst])
            if not is_out:
                dbg_copy(dbgf[l * 6 + 5, :, :], xdst[0:128, OWN:OWN + 256], [R_dst])

    done = False
    astack = ExitStack()
    aw0 = None
    if "mod" in phases and "A" in phases:
        aw0 = alloc_aw(astack)
        load_aw(layers[0], aw0)
    if "mod" in phases:
        phase_mod()
    for l in layers:
        xsrc, R_x = (xT, Res()) if l == 0 else (xl1, R_xl1)
        for ph in phases:
            if done:
                break
            if ph == "A":
                if l == layers[0] and aw0 is not None:
                    phase_A(l, xsrc, R_x, aw0)
                    astack.close()
                else:
                    phase_A(l, xsrc, R_x)
            elif ph == "X":
                phase_X(l)
            elif ph == "ATT":
                c1stack = ExitStack()
                c1w = alloc_c1w(c1stack)
                phase_ATT(l, prefetch=lambda: load_c1w(l, c1w))
            elif ph == "C1":
                phase_C1(l, xsrc, R_x, c1w)
                c1stack.close()
            elif ph == "C2":
                if l == DEPTH - 1:
                    phase_C2(l, outT, R_out, True)
                else:
                    phase_C2(l, xl1, R_xl1, False)
            if stop_after == (l, ph):
                done = True
    if not final_ops:
        final_ops = list(S.last.values())
    else:
        final_ops = final_ops + list(S.last.values())
    S.emit(nc, top, final_ops)
    top.close()
    return nc


def _fm(v, p=128):
    v = np.asarray(v, np.float32)
    return np.ascontiguousarray(v.reshape(-1, p).T)


def rope_tables(half):
    pos = np.arange(OWN) + half * OWN
    row = (pos // 64).astype(np.float32)
    col = (pos % 64).astype(np.float32)
    inv = (1.0 / (10000.0 ** (np.arange(0, 16, 2, dtype=np.float32) / 16.0))).astype(np.float32)
    ar = row[:, None] * inv
    ac = col[:, None] * inv
    ang = np.concatenate([ar, ar, ac, ac], axis=-1)
    cos = np.cos(ang).astype(np.float32)
    sin = np.sin(ang).astype(np.float32)
    sign = np.ones(32, np.float32)
    sign[0:8] = -1.0
    sign[16:24] = -1.0
    sin = sin * sign
    out = np.zeros((2, 128, NTOK), np.float32)
    out[0, :, :OWN] = np.tile(cos.T, (4, 1))
    out[1, :, :OWN] = np.tile(sin.T, (4, 1))
    out[0, :, OWN:] = 1.0
    return out


def pack_vecs(inp):
    vv = np.zeros((DEPTH, 128, NV), np.float32)
    for l in range(DEPTH):
        vv[l, :, V_GPRE1:V_GPRE1 + 8] = _fm(inp["g_pre_mix"][l])
        vv[l, :, V_GPOST1:V_GPOST1 + 8] = _fm(inp["g_post_mix"][l])
        vv[l, :, V_GPRE2:V_GPRE2 + 8] = _fm(inp["g_pre_mlp"][l])
        vv[l, :, V_GPOST2:V_GPOST2 + 8] = _fm(inp["g_post_mlp"][l])
        vv[l, :, V_BMOD:V_BMOD + 48] = _fm(inp["b_mod"][l])
        vv[l, :, V_GQ:V_GQ + 2] = _fm(inp["g_q"][l])
        vv[l, :, V_GKV:V_GKV + 1] = _fm(inp["g_kv"][l])
        vv[l, :, V_DWB:V_DWB + 2] = _fm(inp["conf_dw_b"][l])
        vv[l, :, V_LNG:V_LNG + 2] = _fm(inp["conf_ln_g"][l])
        vv[l, :, V_LNB:V_LNB + 2] = _fm(inp["conf_ln_b"][l])
        gb = np.asarray(inp["g_branch"][l], np.float32)
        vv[l, :, V_GBC:V_GBC + 2] = _fm(gb[512:768])
        vv[l, :, V_GBS:V_GBS + 2] = _fm(gb[768:1024])
        vv[l, 0:64, V_GBA:V_GBA + 8] = gb[0:512].reshape(8, 64).T
        cw = np.asarray(inp["conf_dw_w"][l], np.float32)
        vv[l, :, V_CW:V_CW + 62] = cw.T.reshape(2, 128, 31).transpose(1, 0, 2).reshape(128, 62)
        sw = np.asarray(inp["sc_dw_w"][l], np.float32)
        vv[l, :, V_SW:V_SW + 6] = sw.T.reshape(2, 128, 3).transpose(1, 0, 2).reshape(128, 6)
    return vv


def make_in_maps(inp):
    x = np.asarray(inp["x"], np.float32)
    ctx = np.asarray(inp["ctx"], np.float32)
    c = np.asarray(inp["c"], np.float32)
    c_ctx = np.asarray(inp["c_ctx"], np.float32)
    vv = pack_vecs(inp)
    shared = {k: np.ascontiguousarray(np.asarray(inp[k], np.float32)) for k in
              ("w_mod", "w_in", "w_q_b", "w_kv_b", "w_o", "w_mlp_in", "w_mlp_out")}
    ropes = [rope_tables(0), rope_tables(1)]
    maps = []
    for core in range(8):
        b, half = core // 2, core % 2
        xt = np.empty((D, NTOK), np.float32)
        xt[:, :OWN] = x[b, half * OWN:(half + 1) * OWN, :].T
        xt[:, OWN:] = ctx[b].T
        cv = np.empty((128, 8, 2), np.float32)
        cv[:, :, 0] = _fm(c[b])
        cv[:, :, 1] = _fm(c_ctx)
        cs = np.zeros((128, 130), np.float32)
        cs[:, 0:128] = np.eye(128, dtype=np.float32)
        cs[:, 128] = 1.0 if half == 1 else 0.0
        cs[:, 129] = 1.0 if half == 0 else 0.0
        m = {"xT": xt, "cvecT": cv.reshape(128, 16), "vecs": vv, "rope": ropes[half], "consts": cs}
        m.update(shared)
        maps.append(m)
    return maps


_NC_CACHE = {}


def kernel(**inputs):
    if "nc" not in _NC_CACHE:
        _NC_CACHE["nc"] = build_program()
    nc = _NC_CACHE["nc"]
    maps = make_in_maps(inputs)
    res = run_bass_kernel_spmd(nc, maps, core_ids=list(range(8)))
    out = np.empty((NB, SEQ, D), np.float32)
    for core in range(8):
        b, half = core // 2, core % 2
        out[b, half * OWN:(half + 1) * OWN, :] = np.asarray(res.results[core]["outT"], np.float32).T
    return out
```

```python
import math
from contextlib import ExitStack

import numpy as np
import concourse.bass as bass
import concourse.mybir as mybir
from concourse.bass_utils import run_bass_kernel_spmd

F32 = mybir.dt.float32
F32R = mybir.dt.float32r
BF16 = mybir.dt.bfloat16
AF = mybir.ActivationFunctionType
ALU = mybir.AluOpType

D = 1024
SEQ = 8192
NB = 4
DEPTH = 2
CTX = 256
OWN = 4096
NTOK = OWN + CTX
NKEY = CTX + SEQ
NKT = NKEY // 128
HEADS = 8
IN_W = 1696
DFF = 4096
EPS = 1e-6
SM_SCALE = 1.0 / math.sqrt(96.0)
HALO = 15
LAT0 = HALO
CTX0 = HALO + OWN + HALO + HALO
TP = CTX0 + CTX + HALO

V_GPRE1, V_GPOST1, V_GPRE2, V_GPOST2 = 0, 8, 16, 24
V_BMOD = 32
V_GQ = 80
V_GKV = 82
V_DWB = 83
V_LNG = 85
V_LNB = 87
V_GBC = 89
V_GBS = 91
V_GBA = 93
V_CW = 101
V_SW = 163
NV = 169

SAME_ENG_SYNC = True


class Res:
    __slots__ = ("w", "rs")

    def __init__(self):
        self.w = None
        self.rs = {}


class Op:
    __slots__ = ("eng", "fn", "deps", "sig", "cnt", "chan", "ndma")


class Rec:
    __slots__ = ("name", "args", "kw")

    def __init__(self, name, args, kw):
        self.name, self.args, self.kw = name, args, kw


class _Proxy:
    def __getattr__(self, name):
        return lambda *a, **kw: Rec(name, a, kw)


PROXY = _Proxy()


class Tile:
    def __init__(self, t, nsub=1):
        self.t = t
        self.res = [Res() for _ in range(nsub)]

    def R(self, *idx):
        if not idx:
            return list(self.res)
        return [self.res[i] for i in idx]


class Sched:
    ENGS = ("sp", "pe", "act", "dve", "pool")

    def __init__(self):
        self.ops = {e: [] for e in self.ENGS}
        self.last = {}
        self.pending = {e: [] for e in self.ENGS}
        self.nchan = 0

    def new_chan(self):
        self.nchan += 1
        return "c%d" % self.nchan

    def _key(self, o):
        return o.chan if o.chan is not None else o.eng

    def op(self, eng, fn, reads=(), writes=(), chan=None, ndma=0):
        o = Op()
        o.eng, o.fn, o.chan, o.ndma, o.sig, o.cnt = eng, fn(PROXY), chan, ndma, chan is not None, 0
        deps = {}

        def add(d):
            if d is None:
                return
            if d.chan is None and d.eng == eng and (eng == "pe" or not SAME_ENG_SYNC):
                return
            deps[id(d)] = d

        for r in reads:
            add(r.w)
        for w in writes:
            add(w.w)
            for x in w.rs.values():
                add(x)
        for d in self.pending[eng]:
            add(d)
        self.pending[eng] = []
        o.deps = list(deps.values())
        for d in o.deps:
            d.sig = True
        k = self._key(o)
        for r in reads:
            r.rs[k] = o
        for w in writes:
            w.w = o
            w.rs = {}
        self.ops[eng].append(o)
        self.last[k] = o
        return o

    def barrier(self):
        lasts = list(self.last.values())
        for e in self.ENGS:
            self.pending[e] = list(lasts)

    def emit(self, nc, stack, final_ops):
        for o in final_ops:
            o.sig = True
        cnts = {}
        for e in self.ENGS:
            for o in self.ops[e]:
                k = self._key(o)
                if o.chan is not None:
                    cnts[k] = cnts.get(k, 0) + 16 * o.ndma
                    o.cnt = cnts[k]
                elif o.sig:
                    cnts[k] = cnts.get(k, 0) + 1
                    o.cnt = cnts[k]
        print("n_sems", len(cnts), {e: len(self.ops[e]) for e in self.ENGS})
        sems = {k: stack.enter_context(nc.semaphore("s_" + k)) for k in cnts}
        block = stack.enter_context(nc.Block())

        def run(ename, eng):
            known = {}

            def wait_for(dlist):
                need = {}
                for d in dlist:
                    k = self._key(d)
                    if d.cnt > need.get(k, 0):
                        need[k] = d.cnt
                for k, v in need.items():
                    if known.get(k, 0) < v:
                        eng.wait_ge(sems[k], v)
                        known[k] = v

            for o in self.ops[ename]:
                wait_for(o.deps)
                k = self._key(o)
                if o.chan is not None:
                    assert len(o.fn) == o.ndma, (len(o.fn), o.ndma)
                    for r in o.fn:
                        getattr(eng, r.name)(*r.args, **r.kw).then_inc(sems[k], 16)
                else:
                    r = o.fn
                    ins = getattr(eng, r.name)(*r.args, **r.kw)
                    if o.sig:
                        ins.then_inc(sems[k], 1)
            if ename == "sp":
                wait_for(final_ops)

        block.sync(lambda e: run("sp", e))
        block.tensor(lambda e: run("pe", e))
        block.scalar(lambda e: run("act", e))
        block.vector(lambda e: run("dve", e))
        block.gpsimd(lambda e: run("pool", e))


def build_program(layers=(0, 1), phases=("mod", "A", "X", "ATT", "C1", "C2"), debug_ext=False,
                  stop_after=None):
    nc = bass.Bass("TRN2", target_bir_lowering=False)
    S = Sched()
    top = ExitStack()

    def dram(name, shape, dt, kind=None):
        if kind is None:
            return nc.dram_tensor(name, shape, dt).ap()
        return nc.dram_tensor(name, shape, dt, kind=kind).ap()

    xT = dram("xT", [D, NTOK], F32, "ExternalInput")
    cvecT = dram("cvecT", [128, 16], F32, "ExternalInput")
    vecs = dram("vecs", [DEPTH, 128, NV], F32, "ExternalInput")
    rope = dram("rope", [2, 128, NTOK], F32, "ExternalInput")
    consts = dram("consts", [128, 130], F32, "ExternalInput")
    w_mod = dram("w_mod", [DEPTH, D, 6 * D], F32, "ExternalInput")
    w_in = dram("w_in", [DEPTH, D, IN_W], F32, "ExternalInput")
    w_q_b = dram("w_q_b", [DEPTH, 256, 768], F32, "ExternalInput")
    w_kv_b = dram("w_kv_b", [DEPTH, 128, 1024], F32, "ExternalInput")
    w_o = dram("w_o", [DEPTH, D, D], F32, "ExternalInput")
    w1 = dram("w_mlp_in", [DEPTH, D, DFF], F32, "ExternalInput")
    w2 = dram("w_mlp_out", [DEPTH, DFF, D], F32, "ExternalInput")
    outT = dram("outT", [D, OWN], F32, "ExternalOutput")

    qT = dram("qT", [HEADS, 96, NTOK], BF16)
    lat_ctx = dram("lat_ctx", [160, CTX], BF16)
    convT = dram("convT", [768, TP], BF16)
    attnT = dram("attnT", [HEADS, 64, NTOK], F32)
    xmid = dram("xmid", [D, NTOK], F32)
    xl1 = dram("xl1", [D, NTOK], F32)
    send_lat = nc.dram_tensor("send_lat", [160, OWN], BF16).ap()
    recv_lat = nc.dram_tensor("recv_lat", [320, OWN], BF16).ap()
    send_edge = nc.dram_tensor("send_edge", [512, 2 * HALO], BF16).ap()
    recv_edge = nc.dram_tensor("recv_edge", [1024, 2 * HALO], BF16).ap()
    dbg_lat = None
    dbg_mod = dram("dbg_mod", [128, DEPTH * 6 * 16], F32, "ExternalOutput") if debug_ext else None
    dbgb = dram("dbgb", [DEPTH * 8, 128, 256], BF16, "ExternalOutput") if debug_ext else None
    dbgf = dram("dbgf", [DEPTH * 6, 128, 256], F32, "ExternalOutput") if debug_ext else None

    dbg_chan = S.new_chan()

    def dbg_copy(dst, src, reads):
        o = S.op("sp", lambda e: [e.dma_start(out=dst, in_=src)], reads=reads, chan=dbg_chan, ndma=1)
        final_ops.append(o)

    R_q, R_latctx, R_conv, R_attn, R_xmid, R_xl1, R_out = (Res() for _ in range(7))
    R_send, R_recv, R_sedge, R_redge = Res(), Res(), Res(), Res()

    name_i = [0]

    def sb(stack, name, shape, dt, nsub=1):
        name_i[0] += 1
        return Tile(stack.enter_context(nc.sbuf_tensor("%s_%d" % (name, name_i[0]), shape, dt)), nsub)

    banks = [Tile(top.enter_context(nc.psum_tensor("bank%d" % i, [128, 512], F32)), 2) for i in range(8)]
    bank_i = [0, 0]

    def next_bank():
        b = banks[bank_i[0] % 5]
        bank_i[0] += 1
        return b

    def acc_bank():
        b = banks[5 + bank_i[1] % 3]
        bank_i[1] += 1
        return b

    ones = sb(top, "ones", [128, 128], F32)
    sel = sb(top, "sel", [65, 64], F32)
    eye = sb(top, "eye", [128, 130], F32)
    eyeb = sb(top, "eyeb", [128, 128], BF16)
    vec = [sb(top, "vec%d" % l, [128, NV], F32) for l in range(DEPTH)]
    MOD = [sb(top, "mod%d" % l, [128, 6 * 16], F32) for l in range(DEPTH)]
    zero15 = sb(top, "zero15", [128, HALO], BF16)
    final_ops = []

    S.op("dve", lambda e: e.memset(ones.t[:], 1.0), writes=ones.R())
    sel2 = sb(top, "sel2", [65, 128], F32)
    selr = sb(top, "selr", [65, 128], F32R)
    S.op("dve", lambda e: e.memset(sel2.t[:], 0.0), writes=sel2.R())
    S.op("dve", lambda e: e.memset(sel2.t[64:65, :], 1.0), writes=sel2.R())
    S.op("dve", lambda e: e.tensor_copy(out=selr.t[:], in_=sel2.t[:]), reads=sel2.R(), writes=selr.R())
    onesr = sb(top, "onesr", [128, 128], F32R)
    S.op("dve", lambda e: e.tensor_copy(out=onesr.t[:], in_=ones.t[:]), reads=ones.R(), writes=onesr.R())
    S.op("dve", lambda e: e.memset(sel.t[:], 0.0), writes=sel.R())
    S.op("dve", lambda e: e.memset(sel.t[64:65, :], 1.0), writes=sel.R())
    S.op("dve", lambda e: e.memset(zero15.t[:], 0.0), writes=zero15.R())
    S.op("sp", lambda e: [e.dma_start(out=eye.t[:], in_=consts[:, :])], writes=eye.R(),
         chan=S.new_chan(), ndma=1)
    S.op("dve", lambda e: e.tensor_copy(out=eyeb.t[:], in_=eye.t[:, 0:128]), reads=eye.R(), writes=eyeb.R())
    for l in range(DEPTH):
        S.op("sp", lambda e, l=l: [e.dma_start(out=vec[l].t[:], in_=vecs[l])], writes=vec[l].R(),
             chan=S.new_chan(), ndma=1)

    def mod_ap(l, j, c, s):
        i = j * 16 + c * 2 + s
        return MOD[l].t[:, i:i + 1]

    def phase_mod():
        with ExitStack() as st:
            cv = sb(st, "cv", [128, 16], F32)
            scb = sb(st, "scb", [128, 8, 2], BF16)
            wmf = [sb(st, "wmf%d" % i, [128, 8, 1024], F32, 4) for i in range(2)]
            wm = [sb(st, "wm%d" % i, [128, 8, 1024], BF16, 4) for i in range(2)]
            mraw = sb(st, "mraw", [128, 6 * 16], F32)
            S.op("sp", lambda e: [e.dma_start(out=cv.t[:], in_=cvecT[:, :])], writes=cv.R(),
                 chan=S.new_chan(), ndma=1)
            S.op("act", lambda e: e.activation(out=scb.t[:].rearrange("p c s -> p (c s)"), in_=cv.t[:], func=AF.Silu),
                 reads=cv.R(), writes=scb.R())
            chans = [S.new_chan(), S.new_chan()]
            it = 0
            for l in layers:
                wv = w_mod[l].rearrange("(c p) n -> p c n", p=128)
                bank = acc_bank()
                for j in range(6):
                    w = wm[it % 2]
                    wf = wmf[it % 2]
                    S.op("sp", lambda e, wf=wf, j=j, wv=wv: [
                        e.dma_start(out=wf.t[:, 2 * q:2 * q + 2, :], in_=wv[:, 2 * q:2 * q + 2, j * 1024:(j + 1) * 1024])
                        for q in range(4)], writes=wf.R(), chan=chans[it % 2], ndma=4)
                    for q in range(4):
                        eng = ("dve", "pool", "act", "dve")[q]
                        if eng == "act":
                            S.op("act", lambda e: e.activation(out=w.t[:, 2 * q:2 * q + 2, :], in_=wf.t[:, 2 * q:2 * q + 2, :],
                                                               func=AF.Copy), reads=wf.R(q), writes=w.R(q))
                        else:
                            S.op(eng, lambda e: e.tensor_copy(out=w.t[:, 2 * q:2 * q + 2, :], in_=wf.t[:, 2 * q:2 * q + 2, :]),
                                 reads=wf.R(q), writes=w.R(q))
                    it += 1
                    for oc in range(8):
                        for k in range(8):
                            col = (j * 8 + oc) * 2
                            S.op("pe", lambda e, w=w, oc=oc, k=k, col=col, bank=bank: e.matmul(
                                bank.t[:, col:col + 2], lhsT=w.t[:, k, oc * 128:(oc + 1) * 128],
                                rhs=scb.t[:, k, :], start=(k == 0), stop=(k == 7)),
                                reads=w.R(k // 2) + scb.R(), writes=bank.R())
                for s in range(2):
                    S.op("dve", lambda e, l=l, s=s, bank=bank: e.tensor_tensor(
                        out=mraw.t[:].rearrange("p (q s) -> p q s", s=2)[:, :, s],
                        in0=bank.t[:, 0:96].rearrange("p (q s) -> p q s", s=2)[:, :, s],
                        in1=vec[l].t[:, V_BMOD:V_BMOD + 48], op=ALU.add),
                        reads=bank.R() + vec[l].R(), writes=mraw.R())
                M3 = MOD[l].t[:].rearrange("p (j c s) -> p j c s", j=6, s=2)
                R3 = mraw.t[:].rearrange("p (j c s) -> p j c s", j=6, s=2)
                for s in range(2):
                    for (dst, src_scale, gcol, kind) in ((0, 1, V_GPRE1, "g1"), (2, 2, V_GPOST1, "gp"),
                                                         (3, 4, V_GPRE2, "g1"), (5, 5, V_GPOST2, "gp")):
                        if kind == "g1":
                            S.op("dve", lambda e, l=l, s=s, dst=dst, src=src_scale, gcol=gcol, M3=M3, R3=R3:
                                 e.scalar_tensor_tensor(out=M3[:, dst, :, s], in0=R3[:, src, :, s], scalar=1.0,
                                                        in1=vec[l].t[:, gcol:gcol + 8], op0=ALU.add, op1=ALU.mult),
                                 reads=mraw.R() + vec[l].R(), writes=MOD[l].R())
                        else:
                            S.op("dve", lambda e, l=l, s=s, dst=dst, src=src_scale, gcol=gcol, M3=M3, R3=R3:
                                 e.tensor_tensor(out=M3[:, dst, :, s], in0=R3[:, src, :, s],
                                                 in1=vec[l].t[:, gcol:gcol + 8], op=ALU.mult),
                                 reads=mraw.R() + vec[l].R(), writes=MOD[l].R())
                    for (dst, src) in ((1, 0), (4, 3)):
                        S.op("dve", lambda e, s=s, dst=dst, src=src, M3=M3, R3=R3:
                             e.tensor_copy(out=M3[:, dst, :, s], in_=R3[:, src, :, s]),
                             reads=mraw.R(), writes=MOD[l].R())
                if debug_ext:
                    o = S.op("sp", lambda e, l=l: [e.dma_start(out=dbg_mod[:, l * 96:(l + 1) * 96], in_=MOD[l].t[:])],
                             reads=MOD[l].R(), chan=S.new_chan(), ndma=1)
                    final_ops.append(o)
        S.barrier()

    def emit_rstd(bank, tmp, out, np_, n, dim):
        S.op("act", lambda e: e.activation(out=tmp.t[0:np_, 0:n], in_=bank.t[0:np_, 0:n], func=AF.Sqrt,
                                           bias=epsb.t[0:np_, 0:1], scale=1.0 / dim),
             reads=bank.R() + epsb.R(), writes=tmp.R())
        S.op("dve", lambda e: e.reciprocal(out=out.t[0:np_, 0:n], in_=tmp.t[0:np_, 0:n]),
             reads=tmp.R(), writes=out.R())

    rscr = sb(top, "rscr", [128, 512], F32)
    epsb = sb(top, "epsb", [128, 1], F32)
    S.op("dve", lambda e: e.memset(epsb.t[:], EPS), writes=epsb.R())

    def chunks(n, with_ctx):
        cs = [(t0, n, 0) for t0 in range(0, OWN, n)]
        if with_ctx:
            cs += [(t0, min(n, CTX), 1) for t0 in range(OWN, NTOK, min(n, CTX))]
        return cs

    def conv_col(t0, s):
        return LAT0 + t0 if s == 0 else CTX0 + (t0 - OWN)

    def phase_A(l, xsrc, R_x):
        with ExitStack() as st:
            win = sb(st, "win", [128, 8, IN_W], BF16)
            wkp = sb(st, "wkp", [128, 8, 32], BF16)
            wqn = sb(st, "wqn", [128, 2, 512], BF16)
            wqr = sb(st, "wqr", [128, 2, 256], BF16)
            wqp = sb(st, "wqp", [128, 2, 256], BF16)
            xc = [sb(st, "xc%d" % i, [128, 8, 512], F32, 8) for i in range(2)]
            sq = sb(st, "sqA", [128, 8, 512], F32R, 8)
            tmpn = [sb(st, "tmpn%d" % i, [128, 512], F32) for i in range(2)]
            hT = [sb(st, "hT%d" % i, [128, 8, 512], BF16, 8) for i in range(2)]
            rs_t = sb(st, "rsA_t", [128, 512], F32)
            rs_t2 = sb(st, "rsA_t2", [128, 512], F32)
            rs_n = sb(st, "rsA_n", [128, 512], F32)
            rstd = sb(st, "rstdA", [128, 512], F32)
            rsq = sb(st, "rstdq", [128, 512], F32)
            rskv = sb(st, "rstdkv", [128, 512], F32)
            zq = sb(st, "zq", [128, 2, 512], F32, 2)
            zqs = sb(st, "zqs", [128, 2, 512], F32R, 2)
            zqn = sb(st, "zqn", [128, 2, 512], BF16, 2)
            ckv = sb(st, "ckv", [128, 512], F32)
            ckvs = sb(st, "ckvs", [128, 512], F32R)
            lat = [sb(st, "lat%d" % i, [128, 512], BF16) for i in range(2)]
            kro = [sb(st, "kro%d" % i, [32, 512], BF16) for i in range(2)]
            t1 = sb(st, "t1A", [128, 512], F32)
            t2 = sb(st, "t2A", [128, 512], F32)
            qn = [sb(st, "qn%d" % i, [128, 4, 512], BF16, 4) for i in range(2)]
            qr = [sb(st, "qr%d" % i, [128, 2, 512], BF16, 2) for i in range(2)]
            cs_t = [sb(st, "cos%d" % i, [128, 2, 512], F32) for i in range(2)]
            sig = sb(st, "sig", [128, 512], F32)
            csb = sb(st, "csb", [128, 512], F32)
            upb = [sb(st, "upb%d" % i, [128, 6, 512], BF16, 6) for i in range(2)]

            wv = w_in[l].rearrange("(c p) n -> p c n", p=128)
            ch = S.new_chan()
            S.op("pool", lambda e: [e.dma_start(out=win.t[:, 0:4, :], in_=wv[:, 0:4, :]),
                                    e.dma_start(out=win.t[:, 4:8, :], in_=wv[:, 4:8, :])],
                 writes=win.R(), chan=ch, ndma=2)
            ch = S.new_chan()
            S.op("pool", lambda e: [e.dma_start(out=wkp.t[:, :, d0:d0 + 8], in_=wv[:, :, 384 + s0:384 + s0 + 8])
                                    for (d0, s0) in ((0, 8), (8, 0), (16, 24), (24, 16))],
                 writes=wkp.R(), chan=ch, ndma=4)
            qv = w_q_b[l].rearrange("(c p) (h d) -> p c h d", p=128, d=96)
            ch = S.new_chan()

            def load_wq(e):
                ins = []
                for c in range(2):
                    ins.append(e.dma_start(out=wqn.t[:, c, :].rearrange("p (h d) -> p h d", d=64),
                                           in_=qv[:, c, :, 0:64]))
                    ins.append(e.dma_start(out=wqr.t[:, c, :].rearrange("p (h d) -> p h d", d=32),
                                           in_=qv[:, c, :, 64:96]))
                    for (d0, s0) in ((0, 8), (8, 0), (16, 24), (24, 16)):
                        ins.append(e.dma_start(
                            out=wqp.t[:, c, :].rearrange("p (h d) -> p h d", d=32)[:, :, d0:d0 + 8],
                            in_=qv[:, c, :, 64 + s0:64 + s0 + 8]))
                return ins
            S.op("pool", load_wq, writes=wqn.R() + wqr.R() + wqp.R(), chan=ch, ndma=12)

            xv = xsrc.rearrange("(c p) t -> p c t", p=128)
            cl = chunks(512, True)
            chx = [S.new_chan(), S.new_chan()]
            chc = [S.new_chan(), S.new_chan()]
            ch_lat = [S.new_chan() for _ in range(2)]
            ch_q = [S.new_chan() for _ in range(2)]
            ch_up = [S.new_chan() for _ in range(2)]

            def load_x(i):
                t0, n, s = cl[i]
                S.op("sp", lambda e: [e.dma_start(out=xc[i % 2].t[:, 0:4, 0:n], in_=xv[:, 0:4, t0:t0 + n]),
                                      e.dma_start(out=xc[i % 2].t[:, 4:8, 0:n], in_=xv[:, 4:8, t0:t0 + n])],
                     reads=[R_x], writes=xc[i % 2].R(), chan=chx[i % 2], ndma=2)
                S.op("sp", lambda e: [e.dma_start(out=cs_t[i % 2].t[:, :, 0:n],
                                                  in_=rope[:, :, t0:t0 + n].rearrange("a p t -> p a t"))],
                     writes=cs_t[i % 2].R(), chan=chc[i % 2], ndma=1)

            def norm(i):
                t0, n, s = cl[i]
                X, H = xc[i % 2], hT[i % 2]
                bss = acc_bank()

                def ssx(c):
                    S.op("pe", lambda e: e.matmul(bss.t[:, 0:n], lhsT=onesr.t[:], rhs=sq.t[:, c, 0:n],
                                                  start=(c == 0), stop=(c == 7)), reads=onesr.R() + sq.R(c), writes=bss.R())
                for c in range(8):
                    if c >= 2:
                        ssx(c - 2)
                    S.op("pool", lambda e: e.tensor_tensor(out=sq.t[:, c, 0:n], in0=X.t[:, c, 0:n], in1=X.t[:, c, 0:n],
                                                           op=ALU.mult), reads=X.R(c), writes=sq.R(c))
                ssx(6)
                ssx(7)
                emit_rstd(bss, rs_n, rstd, 128, n, float(D))
                for c in range(8):
                    tm = tmpn[c % 2]
                    S.op("dve", lambda e: e.tensor_tensor(out=tm.t[:, 0:n], in0=X.t[:, c, 0:n], in1=rstd.t[:, 0:n], op=ALU.mult),
                         reads=X.R(c) + rstd.R(), writes=tm.R())
                    S.op("dve", lambda e: e.tensor_scalar(out=H.t[:, c, 0:n], in0=tm.t[:, 0:n], scalar1=mod_ap(l, 0, c, s),
                                                          scalar2=mod_ap(l, 1, c, s), op0=ALU.mult, op1=ALU.add),
                         reads=tm.R() + MOD[l].R(), writes=H.R(c))

            def body(i):
                t0, n, s = cl[i]
                X, H, CS = xc[i % 2], hT[i % 2], cs_t[i % 2]
                need_rest = not (s == 1 and l == DEPTH - 1)

                def proj(cols, wt=None, m=None):
                    b = next_bank()
                    for k in range(8):
                        if wt is None:
                            lhs, rd, mm = win.t[:, k, cols[0]:cols[1]], win.R(), cols[1] - cols[0]
                        else:
                            lhs, rd, mm = wt.t[:, k, :], wt.R(), m
                        S.op("pe", lambda e: e.matmul(b.t[0:mm, 0:n], lhsT=lhs, rhs=H.t[:, k, 0:n], start=(k == 0), stop=(k == 7)),
                             reads=rd + H.R(k), writes=b.R())
                    return b
                for c in range(2):
                    b = proj((c * 128, (c + 1) * 128))
                    S.op("act", lambda e: e.activation(out=zq.t[:, c, 0:n], in_=b.t[:, 0:n], func=AF.Copy),
                         reads=b.R(), writes=zq.R(c))
                    S.op("pool", lambda e: e.tensor_tensor(out=zqs.t[:, c, 0:n], in0=zq.t[:, c, 0:n], in1=zq.t[:, c, 0:n],
                                                           op=ALU.mult), reads=zq.R(c), writes=zqs.R(c))
                b = proj((256, 384))
                S.op("act", lambda e: e.activation(out=ckv.t[:, 0:n], in_=b.t[:, 0:n], func=AF.Copy), reads=b.R(), writes=ckv.R())
                S.op("act", lambda e: e.activation(out=ckvs.t[:, 0:n], in_=ckv.t[:, 0:n], func=AF.Square),
                     reads=ckv.R(), writes=ckvs.R())
                b1 = proj((384, 416))
                b2 = proj(None, wt=wkp, m=32)
                KR = kro[i % 2]
                S.op("dve", lambda e: e.tensor_tensor(out=t1.t[0:32, 0:n], in0=b1.t[0:32, 0:n], in1=CS.t[0:32, 0, 0:n], op=ALU.mult),
                     reads=b1.R() + CS.R(), writes=t1.R())
                S.op("dve", lambda e: e.tensor_tensor(out=t2.t[0:32, 0:n], in0=b2.t[0:32, 0:n], in1=CS.t[0:32, 1, 0:n], op=ALU.mult),
                     reads=b2.R() + CS.R(), writes=t2.R())
                S.op("pool", lambda e: e.tensor_tensor(out=KR.t[:, 0:n], in0=t1.t[0:32, 0:n], in1=t2.t[0:32, 0:n], op=ALU.add),
                     reads=t1.R() + t2.R(), writes=KR.R())
                if need_rest:
                    UP = upb[i % 2]
                    for c in range(2):
                        ba = proj((416 + c * 128, 544 + c * 128))
                        bg = proj((672 + c * 128, 800 + c * 128))
                        S.op("act", lambda e: e.activation(out=sig.t[:, 0:n], in_=bg.t[:, 0:n], func=AF.Sigmoid),
                             reads=bg.R(), writes=sig.R())
                        S.op("dve", lambda e: e.tensor_tensor(out=UP.t[:, c, 0:n], in0=ba.t[:, 0:n], in1=sig.t[:, 0:n], op=ALU.mult),
                             reads=ba.R() + sig.R(), writes=UP.R(c))
                    for c in range(2):
                        bb = proj((928 + c * 128, 1056 + c * 128))
                        bc = proj((1184 + c * 128, 1312 + c * 128))
                        bh = proj((1440 + c * 128, 1568 + c * 128))
                        S.op("act", lambda e: e.activation(out=csb.t[:, 0:n], in_=bc.t[:, 0:n], func=AF.Copy),
                             reads=bc.R(), writes=csb.R())
                        S.op("dve", lambda e: e.tensor_tensor(out=UP.t[:, 2 + c, 0:n], in0=bh.t[:, 0:n], in1=csb.t[:, 0:n], op=ALU.mult),
                             reads=bh.R() + csb.R(), writes=UP.R(2 + c))
                        S.op("act", lambda e: e.activation(out=UP.t[:, 4 + c, 0:n], in_=bb.t[:, 0:n], func=AF.Copy),
                             reads=bb.R(), writes=UP.R(4 + c))
                    cc0 = conv_col(t0, s)

                    def store_up(e):
                        ins = [e.dma_start(out=convT[:, cc0:cc0 + n].rearrange("(c p) t -> p c t", p=128), in_=UP.t[:, :, 0:n])]
                        if s == 0 and t0 == 0:
                            ins.append(e.dma_start(out=send_edge[:, 0:HALO].rearrange("(c p) t -> p c t", p=128),
                                                   in_=UP.t[:, 0:4, 0:HALO]))
                        if s == 0 and t0 + n == OWN:
                            ins.append(e.dma_start(out=send_edge[:, HALO:2 * HALO].rearrange("(c p) t -> p c t", p=128),
                                                   in_=UP.t[:, 0:4, n - HALO:n]))
                        return ins
                    nd = 1 + int(s == 0 and t0 == 0) + int(s == 0 and t0 + n == OWN)
                    S.op("sp", store_up, reads=UP.R(), writes=[R_conv, R_sedge], chan=ch_up[i % 2], ndma=nd)
                bq = acc_bank()
                for c in range(2):
                    S.op("pe", lambda e: e.matmul(bq.t[:, 0:n], lhsT=onesr.t[:], rhs=zqs.t[:, c, 0:n], start=(c == 0), stop=(c == 1)),
                         reads=onesr.R() + zqs.R(c), writes=bq.R())
                bkv = acc_bank()
                S.op("pe", lambda e: e.matmul(bkv.t[:, 0:n], lhsT=onesr.t[:], rhs=ckvs.t[:, 0:n], start=True, stop=True),
                     reads=onesr.R() + ckvs.R(), writes=bkv.R())
                emit_rstd(bq, rs_t, rsq, 128, n, 256.0)
                emit_rstd(bkv, rs_t2, rskv, 128, n, 128.0)
                for c in range(2):
                    S.op("dve", lambda e: e.scalar_tensor_tensor(
                        out=zqn.t[:, c, 0:n], in0=zq.t[:, c, 0:n], scalar=vec[l].t[:, V_GQ + c:V_GQ + c + 1],
                        in1=rsq.t[:, 0:n], op0=ALU.mult, op1=ALU.mult), reads=zq.R(c) + rsq.R() + vec[l].R(), writes=zqn.R(c))
                L = lat[i % 2]
                S.op("dve", lambda e: e.scalar_tensor_tensor(
                    out=L.t[:, 0:n], in0=ckv.t[:, 0:n], scalar=vec[l].t[:, V_GKV:V_GKV + 1], in1=rskv.t[:, 0:n],
                    op0=ALU.mult, op1=ALU.mult), reads=ckv.R() + rskv.R() + vec[l].R(), writes=L.R())
                if s == 0:
                    S.op("sp", lambda e: [e.dma_start(out=send_lat[0:128, t0:t0 + n], in_=L.t[:, 0:n]),
                                          e.dma_start(out=send_lat[128:160, t0:t0 + n], in_=KR.t[:, 0:n])],
                         reads=L.R() + KR.R(), writes=[R_send], chan=ch_lat[i % 2], ndma=2)
                else:
                    S.op("sp", lambda e: [e.dma_start(out=lat_ctx[0:128, t0 - OWN:t0 - OWN + n], in_=L.t[:, 0:n]),
                                          e.dma_start(out=lat_ctx[128:160, t0 - OWN:t0 - OWN + n], in_=KR.t[:, 0:n])],
                         reads=L.R() + KR.R(), writes=[R_latctx], chan=ch_lat[i % 2], ndma=2)
                if i + 1 < len(cl):
                    norm(i + 1)
                if need_rest:
                    QN, QR = qn[i % 2], qr[i % 2]

                    def qproj(wt, g, m=128):
                        b = next_bank()
                        for c in range(2):
                            S.op("pe", lambda e: e.matmul(b.t[0:m, 0:n], lhsT=wt.t[:, c, g * 128:(g + 1) * 128], rhs=zqn.t[:, c, 0:n],
                                                          start=(c == 0), stop=(c == 1)), reads=wt.R() + zqn.R(c), writes=b.R())
                        return b
                    for g in range(4):
                        b = qproj(wqn, g)
                        S.op("act", lambda e: e.activation(out=QN.t[:, g, 0:n], in_=b.t[:, 0:n], func=AF.Copy),
                             reads=b.R(), writes=QN.R(g))
                    for g in range(2):
                        b1 = qproj(wqr, g)
                        b2 = qproj(wqp, g)
                        S.op("dve", lambda e: e.tensor_tensor(out=t1.t[:, 0:n], in0=b1.t[:, 0:n], in1=CS.t[:, 0, 0:n], op=ALU.mult),
                             reads=b1.R() + CS.R(), writes=t1.R())
                        S.op("dve", lambda e: e.tensor_tensor(out=t2.t[:, 0:n], in0=b2.t[:, 0:n], in1=CS.t[:, 1, 0:n], op=ALU.mult),
                             reads=b2.R() + CS.R(), writes=t2.R())
                        S.op("pool", lambda e: e.tensor_tensor(out=QR.t[:, g, 0:n], in0=t1.t[:, 0:n], in1=t2.t[:, 0:n], op=ALU.add),
                             reads=t1.R() + t2.R(), writes=QR.R(g))

                    def store_q(e):
                        ins = []
                        for h in range(HEADS):
                            ins.append(e.dma_start(out=qT[h, 0:64, t0:t0 + n], in_=QN.t[(h % 2) * 64:(h % 2) * 64 + 64, h // 2, 0:n]))
                            ins.append(e.dma_start(out=qT[h, 64:96, t0:t0 + n], in_=QR.t[(h % 4) * 32:(h % 4) * 32 + 32, h // 4, 0:n]))
                        return ins
                    S.op("sp", store_q, reads=QN.R() + QR.R(), writes=[R_q], chan=ch_q[i % 2], ndma=16)

            load_x(0)
            if len(cl) > 1:
                load_x(1)
            norm(0)
            for i in range(len(cl)):
                body(i)
                if i + 2 < len(cl):
                    load_x(i + 2)
        S.barrier()

    def phase_X(l):
        ch = S.new_chan()

        def cc(e):
            i1 = e.collective_compute("AllGather", ALU.bypass, replica_groups=[[0, 1], [2, 3], [4, 5], [6, 7]],
                                      ins=[send_lat], outs=[recv_lat])
            return i1
        o1 = S.op("pool", cc, reads=[R_send], writes=[R_recv])

        def cc2(e):
            return e.collective_compute("AllGather", ALU.bypass, replica_groups=[[0, 1], [2, 3], [4, 5], [6, 7]],
                                        ins=[send_edge], outs=[recv_edge])
        o2 = S.op("pool", cc2, reads=[R_sedge], writes=[R_redge])
        with ExitStack() as st:
            ed = sb(st, "ed", [128, 4, 2 * HALO], BF16)
            ed2 = sb(st, "ed2", [128, 4, 2 * HALO], BF16)
            S.op("sp", lambda e: [
                e.dma_start(out=ed.t[:, :, 0:HALO],
                            in_=recv_edge[0:512, HALO:2 * HALO].rearrange("(c p) t -> p c t", p=128)),
                e.dma_start(out=ed.t[:, :, HALO:2 * HALO],
                            in_=recv_edge[512:1024, 0:HALO].rearrange("(c p) t -> p c t", p=128))],
                reads=[R_redge], writes=ed.R(), chan=ch, ndma=2)
            S.op("dve", lambda e: e.tensor_scalar(out=ed2.t[:, :, 0:HALO], in0=ed.t[:, :, 0:HALO],
                                                  scalar1=eye.t[:, 128:129], scalar2=None, op0=ALU.mult),
                 reads=ed.R() + eye.R(), writes=ed2.R())
            S.op("dve", lambda e: e.tensor_scalar(out=ed2.t[:, :, HALO:2 * HALO], in0=ed.t[:, :, HALO:2 * HALO],
                                                  scalar1=eye.t[:, 129:130], scalar2=None, op0=ALU.mult),
                 reads=ed.R() + eye.R(), writes=ed2.R())
            ch2 = S.new_chan()

            def st_halo(e):
                ins = [
                    e.dma_start(out=convT[0:512, 0:HALO].rearrange("(c p) t -> p c t", p=128), in_=ed2.t[:, :, 0:HALO]),
                    e.dma_start(out=convT[0:512, LAT0 + OWN:LAT0 + OWN + HALO].rearrange("(c p) t -> p c t", p=128),
                                in_=ed2.t[:, :, HALO:2 * HALO])]
                for c in range(4):
                    ins.append(e.dma_start(out=convT[c * 128:(c + 1) * 128, CTX0 - HALO:CTX0], in_=zero15.t[:]))
                    ins.append(e.dma_start(out=convT[c * 128:(c + 1) * 128, CTX0 + CTX:CTX0 + CTX + HALO], in_=zero15.t[:]))
                return ins
            S.op("sp", st_halo, reads=ed2.R() + zero15.R(), writes=[R_conv], chan=ch2, ndma=10)
            S.barrier()
            if debug_ext:
                b0 = l * 8
                dbg_copy(dbgb[b0 + 0, 0:96, :], qT[3, :, 1024:1280], [R_q])
                dbg_copy(dbgb[b0 + 1, 0:96, :], qT[5, :, OWN:OWN + 256], [R_q])
                dbg_copy(dbgb[b0 + 2, :, :], recv_lat[0:128, 512:768], [R_recv])
                dbg_copy(dbgb[b0 + 3, 0:32, :], recv_lat[288:320, 0:256], [R_recv])
                dbg_copy(dbgb[b0 + 4, :, :], lat_ctx[0:128, :], [R_latctx])
                dbg_copy(dbgb[b0 + 5, :, :], convT[0:128, 0:256], [R_conv])
                dbg_copy(dbgb[b0 + 6, :, :], convT[256:384, LAT0 + OWN + HALO - 256:LAT0 + OWN + HALO], [R_conv])
                dbg_copy(dbgb[b0 + 7, :, :], convT[512:640, CTX0:CTX0 + 256], [R_conv])

    def phase_ATT(l, prefetch=None):
        with ExitStack() as st:
            latT = sb(st, "latT", [128, NKEY], BF16)
            wkv = sb(st, "wkv", [128, 1024], BF16)
            KT = [sb(st, "KT%d" % i, [96, NKEY], BF16, 2) for i in range(2)]
            VH = [sb(st, "VH%d" % i, [128, NKT, 65], BF16, 2) for i in range(2)]
            QH = [sb(st, "QH%d" % i, [96, NTOK], BF16) for i in range(2)]
            PT = [sb(st, "PT%d" % i, [128, 512], BF16) for i in range(4)]
            osb = [sb(st, "osb%d" % i, [65, 512], F32) for i in range(2)]
            rden = [sb(st, "rden%d" % i, [64, 512], F32) for i in range(2)]
            att = [sb(st, "att%d" % i, [64, 512], F32) for i in range(2)]
            denr = [sb(st, "denr%d" % i, [65, 512], F32R) for i in range(2)]
            ch = S.new_chan()
            S.op("sp", lambda e: [
                e.dma_start(out=latT.t[:, 0:CTX], in_=lat_ctx[0:128, :]),
                e.dma_start(out=latT.t[:, CTX:CTX + OWN], in_=recv_lat[0:128, :]),
                e.dma_start(out=latT.t[:, CTX + OWN:NKEY], in_=recv_lat[160:288, :])],
                reads=[R_latctx, R_recv], writes=latT.R(), chan=ch, ndma=3)
            for i in range(2):
                ch = S.new_chan()
                S.op("sp", lambda e, i=i: [
                    e.dma_start(out=KT[i].t[64:96, 0:CTX], in_=lat_ctx[128:160, :]),
                    e.dma_start(out=KT[i].t[64:96, CTX:CTX + OWN], in_=recv_lat[128:160, :]),
                    e.dma_start(out=KT[i].t[64:96, CTX + OWN:NKEY], in_=recv_lat[288:320, :])],
                    reads=[R_latctx, R_recv], writes=KT[i].R(1), chan=ch, ndma=3)
                S.op("pool", lambda e, i=i: e.memset(VH[i].t[:, :, 64:65], 1.0), writes=VH[i].R(1))
            ch = S.new_chan()
            S.op("pool", lambda e: [e.dma_start(out=wkv.t[:], in_=w_kv_b[l])], writes=wkv.R(), chan=ch, ndma=1)
            if prefetch is not None:
                prefetch()
            chq = [S.new_chan(), S.new_chan()]
            cho = [S.new_chan(), S.new_chan()]
            qchunks = chunks(512, l != DEPTH - 1)
            pt_i = 0
            ob_i = 0
            pend = []
            kvbank = banks[7]
            obanks = banks[5:7]

            def kv_thunks(h):
                K, V = KT[h % 2], VH[h % 2]
                th = []
                for j in range((NKEY + 511) // 512):
                    def f(j=j):
                        c0 = j * 512
                        w = min(512, NKEY - c0)
                        S.op("pe", lambda e: e.matmul(kvbank.t[0:64, 0:w], lhsT=wkv.t[:, h * 128:h * 128 + 64],
                                                      rhs=latT.t[:, c0:c0 + w], start=True, stop=True),
                             reads=wkv.R() + latT.R(), writes=kvbank.R())
                        S.op("dve", lambda e: e.tensor_copy(out=K.t[0:64, c0:c0 + w], in_=kvbank.t[0:64, 0:w]),
                             reads=kvbank.R(), writes=K.R(0))
                    th.append(f)
                for g0 in range(0, NKT, 8):
                    def f(g0=g0):
                        g1 = min(g0 + 8, NKT)
                        for kt in range(g0, g1):
                            S.op("pe", lambda e: e.matmul(kvbank.t[:, (kt - g0) * 64:(kt - g0 + 1) * 64],
                                                          lhsT=latT.t[:, kt * 128:(kt + 1) * 128],
                                                          rhs=wkv.t[:, h * 128 + 64:h * 128 + 128], start=True, stop=True),
                                 reads=wkv.R() + latT.R(), writes=kvbank.R())
                        S.op("dve", lambda e: e.tensor_copy(
                            out=V.t[:, g0:g1, 0:64], in_=kvbank.t[:, 0:(g1 - g0) * 64].rearrange("p (k d) -> p k d", d=64)),
                            reads=kvbank.R(), writes=V.R(0))
                    th.append(f)
                return th

            for f in kv_thunks(0):
                f()
            pend_kv = []
            it_n = 0
            for h in range(HEADS):
                K = KT[h % 2]
                V = VH[h % 2]
                Q = QH[h % 2]
                if h == 0:
                    S.op("sp", lambda e: [e.dma_start(out=Q.t[:], in_=qT[0])], reads=[R_q], writes=Q.R(),
                         chan=chq[0], ndma=1)
                if h + 1 < HEADS:
                    Qn = QH[(h + 1) % 2]
                    S.op("sp", lambda e: [e.dma_start(out=Qn.t[:], in_=qT[h + 1])], reads=[R_q], writes=Qn.R(),
                         chan=chq[(h + 1) % 2], ndma=1)
                while pend_kv:
                    pend_kv.pop(0)()
                if h + 1 < HEADS:
                    pend_kv = kv_thunks(h + 1)
                for (t0, n, s) in qchunks:
                    kts = list(range(NKT)) if s == 0 else [0, 1]
                    ob = obanks[ob_i % 2]
                    sbanks = {}

                    def qk(kt):
                        b = next_bank()
                        sbanks[kt] = b
                        S.op("pe", lambda e, b=b, kt=kt: e.matmul(
                            b.t[:, 0:n], lhsT=K.t[0:96, kt * 128:(kt + 1) * 128], rhs=Q.t[0:96, t0:t0 + n],
                            start=True, stop=True), reads=K.R() + Q.R(), writes=b.R())
                    LOOK = 3
                    for a in range(min(LOOK, len(kts))):
                        qk(kts[a])
                    for ai, kt in enumerate(kts):
                        b = sbanks.pop(kt)
                        P = PT[pt_i % 4]
                        pt_i += 1
                        S.op("act", lambda e, b=b, P=P: e.activation(out=P.t[:, 0:n], in_=b.t[:, 0:n], func=AF.Exp,
                                                                      scale=SM_SCALE), reads=b.R(), writes=P.R())
                        if ai + LOOK < len(kts):
                            qk(kts[ai + LOOK])
                        if ai == 1 and pend:
                            pend.pop(0)()
                        it_n += 1
                        if it_n % 6 == 0 and pend_kv:
                            pend_kv.pop(0)()
                        S.op("pe", lambda e, P=P, kt=kt, ai=ai: e.matmul(
                            ob.t[0:65, 0:n], lhsT=V.t[:, kt, 0:65], rhs=P.t[:, 0:n],
                            start=(ai == 0), stop=(ai == len(kts) - 1)), reads=V.R() + P.R(), writes=ob.R())
                    OS = osb[ob_i % 2]
                    RD = rden[ob_i % 2]
                    AT = att[ob_i % 2]
                    ob_i += 1
                    S.op("dve", lambda e: e.tensor_copy(out=OS.t[:, 0:n], in_=ob.t[0:65, 0:n]), reads=ob.R(), writes=OS.R())
                    DR = denr[(ob_i - 1) % 2]
                    S.op("dve", lambda e: e.tensor_copy(out=DR.t[64:65, 0:n], in_=ob.t[64:65, 0:n]), reads=ob.R(), writes=DR.R())

                    def tail(OS=OS, RD=RD, AT=AT, DR=DR, n=n, t0=t0, h=h, oi=ob_i):
                        bd = next_bank()
                        S.op("pe", lambda e: e.matmul(bd.t[:, 0:n], lhsT=selr.t[64:65, :], rhs=DR.t[64:65, 0:n], start=True, stop=True),
                             reads=selr.R() + DR.R(), writes=bd.R())
                        S.op("dve", lambda e: e.reciprocal(out=RD.t[:, 0:n], in_=bd.t[0:64, 0:n]), reads=bd.R(), writes=RD.R())
                        S.op("pool", lambda e: e.tensor_tensor(out=AT.t[:, 0:n], in0=OS.t[0:64, 0:n], in1=RD.t[:, 0:n], op=ALU.mult),
                             reads=OS.R() + RD.R(), writes=AT.R())
                        S.op("sp", lambda e: [e.dma_start(out=attnT[h, :, t0:t0 + n], in_=AT.t[:, 0:n])],
                             reads=AT.R(), writes=[R_attn], chan=cho[oi % 2], ndma=1)
                    pend.append(tail)
            while pend:
                pend.pop(0)()
        S.barrier()
        if debug_ext:
            dbg_copy(dbgf[l * 6 + 0, 0:64, :], attnT[2, :, 512:768], [R_attn])
            dbg_copy(dbgf[l * 6 + 1, 0:64, :], attnT[7, :, OWN:OWN + 256], [R_attn])

    def alloc_c1w(stack):
        return (sb(stack, "woa", [64, 8, 1024], BF16), sb(stack, "wob", [128, 4, 1024], BF16),
                sb(stack, "dgc", [128, 62, 128], BF16), sb(stack, "dgs", [128, 6, 128], BF16))

    c1w_chan = S.new_chan()

    def load_c1w(l, c1w):
        woa, wob, dgc, dgs = c1w
        S.op("pool", lambda e: [e.dma_start(out=woa.t[:], in_=w_o[l, 0:512, :].rearrange("(h r) n -> r h n", r=64)),
                                e.dma_start(out=wob.t[:], in_=w_o[l, 512:1024, :].rearrange("(c p) n -> p c n", p=128))],
             writes=woa.R() + wob.R(), chan=c1w_chan, ndma=2)
        for c in range(2):
            for k in range(31):
                S.op("pool", lambda e: e.tensor_scalar(
                    out=dgc.t[:, c * 31 + k, :], in0=eyeb.t[:], scalar1=vec[l].t[:, V_CW + c * 31 + k:V_CW + c * 31 + k + 1],
                    scalar2=None, op0=ALU.mult), reads=eyeb.R() + vec[l].R(), writes=dgc.R())
            for k in range(3):
                S.op("pool", lambda e: e.tensor_scalar(
                    out=dgs.t[:, c * 3 + k, :], in0=eyeb.t[:], scalar1=vec[l].t[:, V_SW + c * 3 + k:V_SW + c * 3 + k + 1],
                    scalar2=None, op0=ALU.mult), reads=eyeb.R() + vec[l].R(), writes=dgs.R())

    def phase_C1(l, xsrc, R_x, c1w):
        NC1 = 256
        n = NC1
        with ExitStack() as st:
            woa, wob, dgc, dgs = c1w

            class B:
                pass
            bufs = []
            for si in range(2):
                b = B()
                b.xc = sb(st, "xc1", [128, 8, n], F32, 8)
                b.at = sb(st, "at", [64, 8, n], F32)
                b.uw = sb(st, "uw", [128, 4, n + 2 * HALO], BF16)
                b.bw = sb(st, "bw", [128, 2, n], BF16)
                b.sqa = sb(st, "sqa", [64, 2, n], F32, 2)
                b.an = sb(st, "an", [64, 8, n], BF16, 8)
                b.tmp = sb(st, "c1tmp", [128, n], F32)
                b.rs = sb(st, "c1rs", [128, n], F32)
                b.v = sb(st, "c1v", [128, 2, n], F32, 2)
                b.vs = sb(st, "c1vs", [128, 2, n], F32R, 2)
                b.mean = sb(st, "c1mean", [128, n], F32)
                b.msq = sb(st, "c1msq", [128, n], F32)
                b.var = sb(st, "c1var", [128, n], F32)
                b.dd = sb(st, "c1d", [128, 2, n], F32, 2)
                b.cn = sb(st, "cn", [128, 2, n], BF16, 2)
                b.sn = sb(st, "sn", [128, 2, n], BF16, 2)
                b.y = sb(st, "c1y", [128, 8, n], F32, 8)
                b.ys = sb(st, "c1ys", [128, 3, n], F32R, 3)
                b.sd = sb(st, "c1sd", [128, 2, n], F32, 2)
                b.svs = sb(st, "c1svs", [128, 2, n], F32R, 2)
                b.sq4 = sb(st, "c1sq4", [64, 4, n], F32R, 4)
                b.rsL = sb(st, "c1rsL", [128, n], F32)
                b.rsS = sb(st, "c1rsS", [128, n], F32)
                b.rsA = sb(st, "c1rsA", [128, n], F32)
                b.tmpS = sb(st, "c1tmpS", [128, n], F32)
                b.tmpA = sb(st, "c1tmpA", [128, n], F32)
                b.rot = banks[si * 4:si * 4 + 2]
                b.acc = banks[si * 4 + 2:si * 4 + 4]
                b.ri = 0
                b.ai = 0
                b.chl = [S.new_chan() for _ in range(4)]
                b.chs = S.new_chan()
                bufs.append(b)

            xv = xsrc.rearrange("(c p) t -> p c t", p=128)
            xo = xmid.rearrange("(c p) t -> p c t", p=128)
            cl = chunks(NC1, l != DEPTH - 1)

            def gen(i, b):
                t0, _, s = cl[i]
                cc0 = conv_col(t0, s)
                X, A, U, Bw = b.xc, b.at, b.uw, b.bw
                tmp, rs, v, vs, mean, msq, var, dd, cn, sn, y, ys, sqa, an = (
                    b.tmp, b.rs, b.v, b.vs, b.mean, b.msq, b.var, b.dd, b.cn, b.sn, b.y, b.ys, b.sqa, b.an)

                def rot():
                    k = b.rot[b.ri % 2]
                    b.ri += 1
                    return k

                def acc():
                    k = b.acc[b.ai % 2]
                    b.ai += 1
                    return k
                S.op("sp", lambda e: [e.dma_start(out=U.t[:, :, 0:n + 2 * HALO],
                                                  in_=convT[0:512, cc0 - HALO:cc0 + n + HALO].rearrange("(c p) t -> p c t", p=128))],
                     reads=[R_conv], writes=U.R(), chan=b.chl[2], ndma=1)
                S.op("sp", lambda e: [e.dma_start(out=Bw.t[:, :, 0:n],
                                                  in_=convT[512:768, cc0:cc0 + n].rearrange("(c p) t -> p c t", p=128))],
                     reads=[R_conv], writes=Bw.R(), chan=b.chl[3], ndma=1)
                S.op("sp", lambda e: [e.dma_start(out=A.t[:, :, 0:n], in_=attnT[:, :, t0:t0 + n].rearrange("h r t -> r h t"))],
                     reads=[R_attn], writes=A.R(), chan=b.chl[1], ndma=1)
                S.op("sp", lambda e: [e.dma_start(out=X.t[:, :, 0:n], in_=xv[:, :, t0:t0 + n])],
                     reads=[R_x], writes=X.R(), chan=b.chl[0], ndma=1)
                yield
                sd, svs, rsL, rsS, rsA, tmpS, tmpA = b.sd, b.svs, b.rsL, b.rsS, b.rsA, b.tmpS, b.tmpA
                for c in range(2):
                    bc = rot()
                    for k in range(31):
                        S.op("pe", lambda e, k=k: e.matmul(bc.t[:, 0:n], lhsT=dgc.t[:, c * 31 + k, :], rhs=U.t[:, c, k:k + n],
                                                           start=(k == 0), stop=(k == 30)), reads=dgc.R() + U.R(), writes=bc.R())
                    S.op("dve", lambda e: e.tensor_scalar(out=v.t[:, c, 0:n], in0=bc.t[:, 0:n],
                                                          scalar1=vec[l].t[:, V_DWB + c:V_DWB + c + 1], scalar2=None, op0=ALU.add),
                         reads=bc.R() + vec[l].R(), writes=v.R(c))
                    S.op("act", lambda e: e.activation(out=vs.t[:, c, 0:n], in_=v.t[:, c, 0:n], func=AF.Square),
                         reads=v.R(c), writes=vs.R(c))
                    yield
                for c in range(2):
                    bc = rot()
                    for k in range(3):
                        S.op("pe", lambda e, k=k: e.matmul(bc.t[:, 0:n], lhsT=dgs.t[:, c * 3 + k, :],
                                                           rhs=U.t[:, 2 + c, HALO - 1 + k:HALO - 1 + k + n],
                                                           start=(k == 0), stop=(k == 2)), reads=dgs.R() + U.R(), writes=bc.R())
                    S.op("dve", lambda e: e.tensor_tensor(out=sd.t[:, c, 0:n], in0=bc.t[:, 0:n], in1=Bw.t[:, c, 0:n],
                                                          op=ALU.mult), reads=bc.R() + Bw.R(), writes=sd.R(c))
                    S.op("pool", lambda e: e.tensor_tensor(out=svs.t[:, c, 0:n], in0=sd.t[:, c, 0:n], in1=sd.t[:, c, 0:n],
                                                           op=ALU.mult), reads=sd.R(c), writes=svs.R(c))
                sq4 = b.sq4
                for h in range(4):
                    S.op("pool" if h % 2 else "act", (lambda e: e.tensor_tensor(
                        out=sq4.t[:, h, 0:n], in0=A.t[:, h, 0:n], in1=A.t[:, h, 0:n], op=ALU.mult)) if h % 2 else
                        (lambda e: e.activation(out=sq4.t[:, h, 0:n], in_=A.t[:, h, 0:n], func=AF.Square)),
                        reads=A.R(), writes=sq4.R(h))
                yield
                b1 = acc()
                b2 = acc()
                for c in range(2):
                    S.op("pe", lambda e: e.matmul(b1.t[:, 0:n], lhsT=ones.t[:], rhs=v.t[:, c, 0:n],
                                                  start=(c == 0), stop=(c == 1)), reads=ones.R() + v.R(c), writes=b1.R())
                for c in range(2):
                    S.op("pe", lambda e: e.matmul(b2.t[:, 0:n], lhsT=onesr.t[:], rhs=vs.t[:, c, 0:n],
                                                  start=(c == 0), stop=(c == 1)), reads=onesr.R() + vs.R(c), writes=b2.R())
                S.op("dve", lambda e: e.tensor_scalar(out=mean.t[:, 0:n], in0=b1.t[:, 0:n], scalar1=1.0 / 256, scalar2=None,
                                                      op0=ALU.mult), reads=b1.R(), writes=mean.R())
                S.op("dve", lambda e: e.tensor_tensor(out=msq.t[:, 0:n], in0=mean.t[:, 0:n], in1=mean.t[:, 0:n], op=ALU.mult),
                     reads=mean.R(), writes=msq.R())
                S.op("dve", lambda e: e.scalar_tensor_tensor(out=var.t[:, 0:n], in0=b2.t[:, 0:n], scalar=1.0 / 256,
                                                             in1=msq.t[:, 0:n], op0=ALU.mult, op1=ALU.subtract),
                     reads=b2.R() + msq.R(), writes=var.R())
                S.op("act", lambda e: e.activation(out=tmp.t[:, 0:n], in_=var.t[:, 0:n], func=AF.Sqrt, bias=epsb.t[:, 0:1],
                                                   scale=1.0), reads=var.R() + epsb.R(), writes=tmp.R())
                S.op("dve", lambda e: e.reciprocal(out=rsL.t[:, 0:n], in_=tmp.t[:, 0:n]), reads=tmp.R(), writes=rsL.R())
                bsa = acc()
                for h in range(4):
                    S.op("pe", lambda e: e.matmul(bsa.t[:, 0:n], lhsT=onesr.t[0:64, :], rhs=sq4.t[:, h, 0:n],
                                                  start=(h == 0), stop=False), reads=onesr.R() + sq4.R(h), writes=bsa.R())
                for h in range(4, 8):
                    S.op("pool" if h % 2 else "act", (lambda e: e.tensor_tensor(
                        out=sq4.t[:, h % 4, 0:n], in0=A.t[:, h, 0:n], in1=A.t[:, h, 0:n], op=ALU.mult)) if h % 2 else
                        (lambda e: e.activation(out=sq4.t[:, h % 4, 0:n], in_=A.t[:, h, 0:n], func=AF.Square)),
                        reads=A.R(), writes=sq4.R(h % 4))
                yield
                b4 = acc()
                for c in range(2):
                    S.op("pe", lambda e: e.matmul(b4.t[:, 0:n], lhsT=onesr.t[:], rhs=svs.t[:, c, 0:n],
                                                  start=(c == 0), stop=(c == 1)), reads=onesr.R() + svs.R(c), writes=b4.R())
                for c in range(2):
                    S.op("dve", lambda e: e.tensor_tensor(out=dd.t[:, c, 0:n], in0=v.t[:, c, 0:n], in1=mean.t[:, 0:n],
                                                          op=ALU.subtract), reads=v.R(c) + mean.R(), writes=dd.R(c))
                    S.op("dve", lambda e: e.scalar_tensor_tensor(
                        out=dd.t[:, c, 0:n], in0=dd.t[:, c, 0:n], scalar=vec[l].t[:, V_LNG + c:V_LNG + c + 1],
                        in1=rsL.t[:, 0:n], op0=ALU.mult, op1=ALU.mult), reads=dd.R(c) + rsL.R() + vec[l].R(), writes=dd.R(c))
                    S.op("act", lambda e: e.activation(out=v.t[:, c, 0:n], in_=dd.t[:, c, 0:n], func=AF.Silu,
                                                       bias=vec[l].t[:, V_LNB + c:V_LNB + c + 1], scale=1.0),
                         reads=dd.R(c) + vec[l].R(), writes=v.R(c))
                    S.op("act", lambda e: e.activation(out=vs.t[:, c, 0:n], in_=v.t[:, c, 0:n], func=AF.Square),
                         reads=v.R(c), writes=vs.R(c))
                emit_rstd(b4, tmpS, rsS, 128, n, 256.0)
                for c in range(2):
                    S.op("dve", lambda e: e.scalar_tensor_tensor(
                        out=sn.t[:, c, 0:n], in0=sd.t[:, c, 0:n], scalar=vec[l].t[:, V_GBS + c:V_GBS + c + 1],
                        in1=rsS.t[:, 0:n], op0=ALU.mult, op1=ALU.mult), reads=sd.R(c) + rsS.R() + vec[l].R(), writes=sn.R(c))
                for h in range(4, 8):
                    S.op("pe", lambda e: e.matmul(bsa.t[:, 0:n], lhsT=onesr.t[0:64, :], rhs=sq4.t[:, h % 4, 0:n],
                                                  start=False, stop=(h == 7)), reads=onesr.R() + sq4.R(h % 4), writes=bsa.R())
                emit_rstd(bsa, tmpA, rsA, 64, n, 512.0)
                for h in range(8):
                    S.op("dve", lambda e: e.scalar_tensor_tensor(
                        out=an.t[:, h, 0:n], in0=A.t[:, h, 0:n], scalar=vec[l].t[0:64, V_GBA + h:V_GBA + h + 1],
                        in1=rsA.t[0:64, 0:n], op0=ALU.mult, op1=ALU.mult), reads=A.R() + rsA.R() + vec[l].R(), writes=an.R(h))
                yield
                b3 = acc()
                for c in range(2):
                    S.op("pe", lambda e: e.matmul(b3.t[:, 0:n], lhsT=onesr.t[:], rhs=vs.t[:, c, 0:n],
                                                  start=(c == 0), stop=(c == 1)), reads=onesr.R() + vs.R(c), writes=b3.R())
                emit_rstd(b3, tmp, rs, 128, n, 256.0)
                for c in range(2):
                    S.op("dve", lambda e: e.scalar_tensor_tensor(
                        out=cn.t[:, c, 0:n], in0=v.t[:, c, 0:n], scalar=vec[l].t[:, V_GBC + c:V_GBC + c + 1],
                        in1=rs.t[:, 0:n], op0=ALU.mult, op1=ALU.mult), reads=v.R(c) + rs.R() + vec[l].R(), writes=cn.R(c))
                yield
                bsy = acc()

                def ssy(o2):
                    S.op("pe", lambda e: e.matmul(bsy.t[:, 0:n], lhsT=onesr.t[:], rhs=ys.t[:, o2 % 3, 0:n],
                                                  start=(o2 == 0), stop=(o2 == 7)), reads=onesr.R() + ys.R(o2 % 3), writes=bsy.R())
                for oc in range(8):
                    bo = rot()
                    for h in range(8):
                        S.op("pe", lambda e: e.matmul(bo.t[:, 0:n], lhsT=woa.t[:, h, oc * 128:(oc + 1) * 128], rhs=an.t[:, h, 0:n],
                                                      start=(h == 0), stop=False), reads=woa.R() + an.R(h), writes=bo.R())
                    for c in range(2):
                        S.op("pe", lambda e: e.matmul(bo.t[:, 0:n], lhsT=wob.t[:, 2 + c, oc * 128:(oc + 1) * 128], rhs=sn.t[:, c, 0:n],
                                                      start=False, stop=False), reads=wob.R() + sn.R(c), writes=bo.R())
                    for c in range(2):
                        S.op("pe", lambda e: e.matmul(bo.t[:, 0:n], lhsT=wob.t[:, c, oc * 128:(oc + 1) * 128], rhs=cn.t[:, c, 0:n],
                                                      start=False, stop=(c == 1)), reads=wob.R() + cn.R(c), writes=bo.R())
                    if oc >= 2:
                        ssy(oc - 2)
                    S.op("act", lambda e: e.activation(out=y.t[:, oc, 0:n], in_=bo.t[:, 0:n], func=AF.Copy),
                         reads=bo.R(), writes=y.R(oc))
                    S.op("pool" if oc % 2 else "act", (lambda e: e.tensor_tensor(
                        out=ys.t[:, oc % 3, 0:n], in0=y.t[:, oc, 0:n], in1=y.t[:, oc, 0:n], op=ALU.mult)) if oc % 2 else
                        (lambda e: e.activation(out=ys.t[:, oc % 3, 0:n], in_=y.t[:, oc, 0:n], func=AF.Square)),
                        reads=y.R(oc), writes=ys.R(oc % 3))
                    if oc % 4 == 3:
                        yield
                ssy(6)
                ssy(7)
                emit_rstd(bsy, tmp, rs, 128, n, float(D))
                for oc in range(8):
                    S.op("dve", lambda e: e.scalar_tensor_tensor(
                        out=y.t[:, oc, 0:n], in0=y.t[:, oc, 0:n], scalar=mod_ap(l, 2, oc, s), in1=rs.t[:, 0:n],
                        op0=ALU.mult, op1=ALU.mult), reads=y.R(oc) + rs.R() + MOD[l].R(), writes=y.R(oc))
                    S.op("pool", lambda e: e.tensor_tensor(out=X.t[:, oc, 0:n], in0=X.t[:, oc, 0:n], in1=y.t[:, oc, 0:n],
                                                           op=ALU.add), reads=X.R(oc) + y.R(oc), writes=X.R(oc))
                S.op("pool", lambda e: [e.dma_start(out=xo[:, :, t0:t0 + n], in_=X.t[:, :, 0:n])],
                     reads=X.R(), writes=[R_xmid], chan=b.chs, ndma=1)

            active = []
            nxt = 0
            while active or nxt < len(cl):
                while len(active) < 2 and nxt < len(cl):
                    active.append(gen(nxt, bufs[nxt % 2]))
                    nxt += 1
                    if len(active) == 2 and nxt == 2:
                        for _ in range(3):
                            next(active[0])
                for g in list(active):
                    try:
                        next(g)
                    except StopIteration:
                        active.remove(g)
        S.barrier()
        if debug_ext:
            dbg_copy(dbgf[l * 6 + 2, :, :], xmid[128:256, 1024:1280], [R_xmid])
            dbg_copy(dbgf[l * 6 + 3, :, :], xmid[0:128, OWN:OWN + 256], [R_xmid])

    c2_chans = [S.new_chan() for _ in range(8)]

    def phase_C2(l, xdst, R_dst, is_out):
        NC2 = 256
        with ExitStack() as st:
            w1s = sb(st, "w1s", [128, 8, DFF], BF16, 4)
            w2s = sb(st, "w2s", [128, 32, D], BF16, 4)
            xc = [sb(st, "xc2_%d" % i, [128, 8, NC2], F32, 8) for i in range(3)]
            sqP = sb(st, "c2sqP", [128, 3, NC2], F32R, 3)
            sqE = sb(st, "c2sqE", [128, 2, NC2], F32R, 2)
            tmpn = [sb(st, "c2tn%d" % i, [128, NC2], F32) for i in range(2)]
            tme = [sb(st, "c2te%d" % i, [128, NC2], F32) for i in range(2)]
            tmpP = sb(st, "c2tmpP", [128, NC2], F32)
            rsP = sb(st, "c2rsP", [128, NC2], F32)
            tmpE = sb(st, "c2tmpE", [128, NC2], F32)
            rsE = sb(st, "c2rsE", [128, NC2], F32)
            h2 = [sb(st, "h2_%d" % i, [128, 8, NC2], BF16, 8) for i in range(2)]
            rl = [sb(st, "rl%d" % i, [128, NC2], F32) for i in range(2)]
            a2 = sb(st, "a2", [128, 32, NC2], BF16, 32)
            w1v = w1[l].rearrange("(c p) n -> p c n", p=128)
            w2v = w2[l].rearrange("(c p) n -> p c n", p=128)
            for g in range(4):
                S.op("pool", lambda e: [e.dma_start(out=w1s.t[:, :, (2 * g + q) * 512:(2 * g + q + 1) * 512],
                                                    in_=w1v[:, :, (2 * g + q) * 512:(2 * g + q + 1) * 512]) for q in range(2)],
                     writes=w1s.R(g), chan=c2_chans[g], ndma=2)
            for g in range(4):
                S.op("pool", lambda e: [e.dma_start(out=w2s.t[:, (2 * g + q) * 4:(2 * g + q + 1) * 4, :],
                                                    in_=w2v[:, (2 * g + q) * 4:(2 * g + q + 1) * 4, :]) for q in range(2)],
                     writes=w2s.R(g), chan=c2_chans[4 + g], ndma=2)
            xv = xmid.rearrange("(c p) t -> p c t", p=128)
            xo = xdst.rearrange("(c p) t -> p c t", p=128)
            cl = chunks(NC2, l != DEPTH - 1)
            NCH = len(cl)
            chl = [S.new_chan() for _ in range(3)]
            chs = [S.new_chan() for _ in range(3)]
            ybank = banks[0:4]
            hbank = banks[4:6]
            ssPb, ssEb = banks[6], banks[7]
            n = NC2
            bg = []

            def pump(k):
                for _ in range(k):
                    if bg:
                        bg.pop(0)()

            def flush():
                while bg:
                    bg.pop(0)()

            def load(i):
                t0 = cl[i][0]
                X = xc[i % 3]
                S.op("sp", lambda e: [e.dma_start(out=X.t[:, :, 0:n], in_=xv[:, :, t0:t0 + n])],
                     reads=[R_xmid], writes=X.R(), chan=chl[i % 3], ndma=1)

            def prologue(i):
                t0, _, s = cl[i]
                X, H = xc[i % 3], h2[i % 2]
                th = []

                def ssp(c):
                    S.op("pe", lambda e: e.matmul(ssPb.t[:, 0:n], lhsT=onesr.t[:], rhs=sqP.t[:, c % 3, :],
                                                  start=(c == 0), stop=(c == 7)), reads=onesr.R() + sqP.R(c % 3), writes=ssPb.R())
                for c in range(8):
                    def f(c=c):
                        if c >= 2:
                            ssp(c - 2)
                        S.op("pool", lambda e: e.tensor_tensor(out=sqP.t[:, c % 3, :], in0=X.t[:, c, :], in1=X.t[:, c, :],
                                                               op=ALU.mult), reads=X.R(c), writes=sqP.R(c % 3))
                    th.append(f)
                th.append(lambda: (ssp(6), ssp(7)))
                th.append(lambda: emit_rstd(ssPb, tmpP, rsP, 128, n, float(D)))
                for c in range(8):
                    def f(c=c):
                        tm = tmpn[c % 2]
                        S.op("dve", lambda e: e.tensor_tensor(out=tm.t[:, :], in0=X.t[:, c, :], in1=rsP.t[:, :], op=ALU.mult),
                             reads=X.R(c) + rsP.R(), writes=tm.R())
                        S.op("dve", lambda e: e.tensor_scalar(out=H.t[:, c, :], in0=tm.t[:, :], scalar1=mod_ap(l, 3, c, s),
                                                              scalar2=mod_ap(l, 4, c, s), op0=ALU.mult, op1=ALU.add),
                             reads=tm.R() + MOD[l].R(), writes=H.R(c))
                    th.append(f)
                return th

            def epilogue(i):
                t0, _, s = cl[i]
                X = xc[i % 3]
                th = [lambda: emit_rstd(ssEb, tmpE, rsE, 128, n, float(D))]
                for oc in range(8):
                    def f(oc=oc):
                        yb = ybank[oc // 2]
                        c0 = (oc % 2) * n
                        tm = tme[oc % 2]
                        S.op("dve", lambda e: e.scalar_tensor_tensor(
                            out=tm.t[:, :], in0=yb.t[:, c0:c0 + n], scalar=mod_ap(l, 5, oc, s), in1=rsE.t[:, :],
                            op0=ALU.mult, op1=ALU.mult), reads=yb.R() + rsE.R() + MOD[l].R(), writes=tm.R())
                        S.op("pool", lambda e: e.tensor_tensor(out=X.t[:, oc, :], in0=X.t[:, oc, :], in1=tm.t[:, :], op=ALU.add),
                             reads=X.R(oc) + tm.R(), writes=X.R(oc))
                    th.append(f)

                def st_():
                    o = S.op("sp", lambda e: [e.dma_start(out=xo[:, :, t0:t0 + n], in_=X.t[:, :, :])],
                             reads=X.R(), writes=[R_dst], chan=chs[i % 3], ndma=1)
                    if is_out:
                        final_ops.append(o)
                th.append(st_)
                return th

            def main(i):
                H = h2[i % 2]
                for hc in range(32):
                    hb = hbank[hc % 2]
                    for k in range(8):
                        S.op("pe", lambda e, k=k: e.matmul(hb.t[:, 0:n], lhsT=w1s.t[:, k, hc * 128:(hc + 1) * 128],
                                                           rhs=H.t[:, k, :], start=(k == 0), stop=(k == 7)),
                             reads=w1s.R(hc // 8) + H.R(k), writes=hb.R())
                    r = rl[hc % 2]
                    S.op("act", lambda e: e.activation(out=r.t[:, :], in_=hb.t[:, 0:n], func=AF.Relu),
                         reads=hb.R(), writes=r.R())
                    S.op("pool" if hc % 2 else "dve", lambda e: e.tensor_tensor(
                        out=a2.t[:, hc, :], in0=r.t[:, :], in1=r.t[:, :], op=ALU.mult), reads=r.R(), writes=a2.R(hc))
                    pump(1)
                while bg and getattr(bg[0], "_epi", False):
                    bg.pop(0)()

                def ss_mm(oc):
                    S.op("pe", lambda e: e.matmul(ssEb.t[:, 0:n], lhsT=onesr.t[:], rhs=sqE.t[:, oc % 2, :],
                                                  start=(oc == 0), stop=(oc == 7)), reads=onesr.R() + sqE.R(oc % 2), writes=ssEb.R())
                for oc in range(8):
                    yb = ybank[oc // 2]
                    c0 = (oc % 2) * n
                    for hc in range(32):
                        S.op("pe", lambda e, hc=hc: e.matmul(yb.t[:, c0:c0 + n], lhsT=w2s.t[:, hc, oc * 128:(oc + 1) * 128],
                                                             rhs=a2.t[:, hc, :], start=(hc == 0), stop=(hc == 31)),
                             reads=w2s.R(hc // 8) + a2.R(hc), writes=yb.R())
                    if oc % 2 == 1:
                        if oc >= 3:
                            ss_mm(oc - 3)
                            ss_mm(oc - 2)
                        for o2 in (oc - 1, oc):
                            cc = (o2 % 2) * n
                            S.op("act", lambda e, o2=o2, cc=cc: e.activation(out=sqE.t[:, o2 % 2, :], in_=yb.t[:, cc:cc + n],
                                                                             func=AF.Square), reads=yb.R(), writes=sqE.R(o2 % 2))
                    pump(2)
                ss_mm(6)
                ss_mm(7)
                flush()

            load(0)
            if NCH > 1:
                load(1)
            for f in prologue(0):
                f()
            for i in range(NCH):
                if i + 1 < NCH:
                    bg.extend(prologue(i + 1))
                main(i)
                ep = epilogue(i)
                for f in ep:
                    pass
                marked = []
                for f in ep:
                    def g(f=f):
                        f()
                    g._epi = True
                    marked.append(g)
                bg[0:0] = marked
                if i + 2 < NCH:
                    load(i + 2)
            flush()
        S.barrier()
        if debug_ext:
            dbg_copy(dbgf[l * 6 + 4, :, :], xdst[256:384, 2048:2304], [R_dst])
            if not is_out:
                dbg_copy(dbgf[l * 6 + 5, :, :], xdst[0:128, OWN:OWN + 256], [R_dst])

    done = False
    if "mod" in phases:
        phase_mod()
    for l in layers:
        xsrc, R_x = (xT, Res()) if l == 0 else (xl1, R_xl1)
        for ph in phases:
            if done:
                break
            if ph == "A":
                phase_A(l, xsrc, R_x)
            elif ph == "X":
                phase_X(l)
            elif ph == "ATT":
                c1stack = ExitStack()
                c1w = alloc_c1w(c1stack)
                phase_ATT(l, prefetch=lambda: load_c1w(l, c1w))
            elif ph == "C1":
                phase_C1(l, xsrc, R_x, c1w)
                c1stack.close()
            elif ph == "C2":
                if l == DEPTH - 1:
                    phase_C2(l, outT, R_out, True)
                else:
                    phase_C2(l, xl1, R_xl1, False)
            if stop_after == (l, ph):
                done = True
    if not final_ops:
        final_ops = list(S.last.values())
    else:
        final_ops = final_ops + list(S.last.values())
    S.emit(nc, top, final_ops)
    top.close()
    return nc


def _fm(v, p=128):
    v = np.asarray(v, np.float32)
    return np.ascontiguousarray(v.reshape(-1, p).T)


def rope_tables(half):
    pos = np.arange(OWN) + half * OWN
    row = (pos // 64).astype(np.float32)
    col = (pos % 64).astype(np.float32)
    inv = (1.0 / (10000.0 ** (np.arange(0, 16, 2, dtype=np.float32) / 16.0))).astype(np.float32)
    ar = row[:, None] * inv
    ac = col[:, None] * inv
    ang = np.concatenate([ar, ar, ac, ac], axis=-1)
    cos = np.cos(ang).astype(np.float32)
    sin = np.sin(ang).astype(np.float32)
    sign = np.ones(32, np.float32)
    sign[0:8] = -1.0
    sign[16:24] = -1.0
    sin = sin * sign
    out = np.zeros((2, 128, NTOK), np.float32)
    out[0, :, :OWN] = np.tile(cos.T, (4, 1))
    out[1, :, :OWN] = np.tile(sin.T, (4, 1))
    out[0, :, OWN:] = 1.0
    return out


def pack_vecs(inp):
    vv = np.zeros((DEPTH, 128, NV), np.float32)
    for l in range(DEPTH):
        vv[l, :, V_GPRE1:V_GPRE1 + 8] = _fm(inp["g_pre_mix"][l])
        vv[l, :, V_GPOST1:V_GPOST1 + 8] = _fm(inp["g_post_mix"][l])
        vv[l, :, V_GPRE2:V_GPRE2 + 8] = _fm(inp["g_pre_mlp"][l])
        vv[l, :, V_GPOST2:V_GPOST2 + 8] = _fm(inp["g_post_mlp"][l])
        vv[l, :, V_BMOD:V_BMOD + 48] = _fm(inp["b_mod"][l])
        vv[l, :, V_GQ:V_GQ + 2] = _fm(inp["g_q"][l])
        vv[l, :, V_GKV:V_GKV + 1] = _fm(inp["g_kv"][l])
        vv[l, :, V_DWB:V_DWB + 2] = _fm(inp["conf_dw_b"][l])
        vv[l, :, V_LNG:V_LNG + 2] = _fm(inp["conf_ln_g"][l])
        vv[l, :, V_LNB:V_LNB + 2] = _fm(inp["conf_ln_b"][l])
        gb = np.asarray(inp["g_branch"][l], np.float32)
        vv[l, :, V_GBC:V_GBC + 2] = _fm(gb[512:768])
        vv[l, :, V_GBS:V_GBS + 2] = _fm(gb[768:1024])
        vv[l, 0:64, V_GBA:V_GBA + 8] = gb[0:512].reshape(8, 64).T
        cw = np.asarray(inp["conf_dw_w"][l], np.float32)
        vv[l, :, V_CW:V_CW + 62] = cw.T.reshape(2, 128, 31).transpose(1, 0, 2).reshape(128, 62)
        sw = np.asarray(inp["sc_dw_w"][l], np.float32)
        vv[l, :, V_SW:V_SW + 6] = sw.T.reshape(2, 128, 3).transpose(1, 0, 2).reshape(128, 6)
    return vv


def make_in_maps(inp):
    x = np.asarray(inp["x"], np.float32)
    ctx = np.asarray(inp["ctx"], np.float32)
    c = np.asarray(inp["c"], np.float32)
    c_ctx = np.asarray(inp["c_ctx"], np.float32)
    vv = pack_vecs(inp)
    shared = {k: np.ascontiguousarray(np.asarray(inp[k], np.float32)) for k in
              ("w_mod", "w_in", "w_q_b", "w_kv_b", "w_o", "w_mlp_in", "w_mlp_out")}
    ropes = [rope_tables(0), rope_tables(1)]
    maps = []
    for core in range(8):
        b, half = core // 2, core % 2
        xt = np.empty((D, NTOK), np.float32)
        xt[:, :OWN] = x[b, half * OWN:(half + 1) * OWN, :].T
        xt[:, OWN:] = ctx[b].T
        cv = np.empty((128, 8, 2), np.float32)
        cv[:, :, 0] = _fm(c[b])
        cv[:, :, 1] = _fm(c_ctx)
        cs = np.zeros((128, 130), np.float32)
        cs[:, 0:128] = np.eye(128, dtype=np.float32)
        cs[:, 128] = 1.0 if half == 1 else 0.0
        cs[:, 129] = 1.0 if half == 0 else 0.0
        m = {"xT": xt, "cvecT": cv.reshape(128, 16), "vecs": vv, "rope": ropes[half], "consts": cs}
        m.update(shared)
        maps.append(m)
    return maps


_NC_CACHE = {}


def kernel(**inputs):
    if "nc" not in _NC_CACHE:
        _NC_CACHE["nc"] = build_program()
    nc = _NC_CACHE["nc"]
    maps = make_in_maps(inputs)
    res = run_bass_kernel_spmd(nc, maps, core_ids=list(range(8)))
    out = np.empty((NB, SEQ, D), np.float32)
    for core in range(8):
        b, half = core // 2, core % 2
        out[b, half * OWN:(half + 1) * OWN, :] = np.asarray(res.results[core]["outT"], np.float32).T
    return out
```
